# Optimizing a Trainium2 kernel written in Bass

```python
import math
import jax
import jax.numpy as jnp
from jax import lax
import numpy as np

D_MODEL = 2048
BATCH = 1
SEQ = 8192
DEPTH = 4

GLA_HEADS = 8
GLA_DK = 64
GLA_DV = 128
GLA_GATE_RANK = 16
GLA_GATE_NORM = 16.0
MOBA_HEADS = 8
MOBA_HEAD_DIM = 128
MOBA_BLOCK = 256
MOBA_TOPK = 3
MOBA_Q_CHUNK = 64
GDN_HEADS = 8
GDN_HEAD_DIM = 128
GDN_CONV = 4
S5_CHANNELS = 1024
S5_GROUP_CH = 16
S5_GROUPS = S5_CHANNELS // S5_GROUP_CH
S5_STATE = 64
LIN_CHUNK = 64
D_FF = 5632
FFN_CONV = 3
DEEPNORM_ALPHA = (2 * DEPTH) ** 0.25
DEEPNORM_BETA = (8 * DEPTH) ** -0.25
N_EVEN = (DEPTH + 1) // 2
N_ODD = DEPTH // 2
EVEN_SIZES = (GLA_HEADS * GLA_DK, GLA_HEADS * GLA_DK, GLA_HEADS * GLA_DV, GLA_GATE_RANK, GLA_HEADS * GLA_DV,
              MOBA_HEADS * MOBA_HEAD_DIM, MOBA_HEADS * MOBA_HEAD_DIM, MOBA_HEADS * MOBA_HEAD_DIM)
ODD_SIZES = (3 * GDN_HEADS * GDN_HEAD_DIM, GDN_HEADS * GDN_HEAD_DIM, GDN_HEADS, GDN_HEADS, S5_CHANNELS)
EVEN_IN = sum(EVEN_SIZES)
ODD_IN = sum(ODD_SIZES)
EVEN_OUT = GLA_HEADS * GLA_DV + MOBA_HEADS * MOBA_HEAD_DIM
ODD_OUT = GDN_HEADS * GDN_HEAD_DIM + S5_CHANNELS
LN_EPS = 1e-5
RMS_EPS = 1e-6

kernel_name = "hybrid_gla_moba_gdn_s5_deepnorm"


def split_cols(h, sizes):
    return jnp.split(h, np.cumsum(sizes)[:-1].tolist(), axis=-1)


def layer_norm(x, g, b):
    xf = x.astype(jnp.float32)
    mu = jnp.mean(xf, -1, keepdims=True)
    var = jnp.mean(jnp.square(xf - mu), -1, keepdims=True)
    return ((xf - mu) * lax.rsqrt(var + LN_EPS) * g.astype(jnp.float32) + b.astype(jnp.float32)).astype(x.dtype)


def rms_norm(x, g):
    return x * lax.rsqrt(jnp.mean(jnp.square(x), -1, keepdims=True) + RMS_EPS) * g.astype(jnp.float32)


def l2_normalize(x):
    return x * lax.rsqrt(jnp.sum(jnp.square(x), -1, keepdims=True) + RMS_EPS)


def causal_depthwise_conv(x, w):
    width, seq = w.shape[0], x.shape[1]
    xp = jnp.pad(x, ((0, 0), (width - 1, 0), (0, 0)))
    return sum(xp[:, j:j + seq] * w[j] for j in range(width))


def to_chunks(t, chunk):
    b, s, h = t.shape[:3]
    t = jnp.moveaxis(t, 2, 1)
    return t.reshape((b, h, s // chunk, chunk) + t.shape[3:])


def alibi_slopes(n_heads):
    return jnp.exp2(-8.0 * jnp.arange(1, n_heads + 1, dtype=jnp.float32) / n_heads)


def gla_chunked(q, k, v, log_a):
    bsz, seq, n_heads, dk = q.shape
    dv = v.shape[-1]
    q, k, v, log_a = (to_chunks(t, LIN_CHUNK) for t in (q, k, v, log_a))
    cum = jnp.cumsum(log_a, axis=3)
    q_e = q * jnp.exp(cum)
    k_e = k * jnp.exp(-cum)
    causal = jnp.tril(jnp.ones((LIN_CHUNK, LIN_CHUNK), bool))
    attn = jnp.where(causal, jnp.einsum('bhncd,bhnsd->bhncs', q_e, k_e), 0.0)
    cum_last = cum[:, :, :, -1]
    kv = jnp.einsum('bhncd,bhncv->bhndv', k * jnp.exp(cum_last[:, :, :, None] - cum), v)

    def step(state, inp):
        decay, kv_n = inp
        return state * decay[..., None] + kv_n, state

    _, s_prev = lax.scan(step, jnp.zeros((bsz, n_heads, dk, dv), jnp.float32),
                         (jnp.moveaxis(jnp.exp(cum_last), 2, 0), jnp.moveaxis(kv, 2, 0)))
    o = jnp.einsum('bhncs,bhnsv->bhncv', attn, v) + jnp.einsum('bhncd,nbhdv->bhncv', q_e, s_prev)
    return o.transpose(0, 2, 3, 1, 4).reshape(bsz, seq, n_heads, dv)


def gated_delta_chunked(q, k, v, g, beta):
    bsz, seq, n_heads, dk = q.shape
    dv = v.shape[-1]
    q, k, v, g, beta = (to_chunks(t, LIN_CHUNK) for t in (q * dk ** -0.5, k, v, g, beta))
    gc = jnp.cumsum(g, axis=-1)
    causal = jnp.tril(jnp.ones((LIN_CHUNK, LIN_CHUNK), bool))
    strict = jnp.tril(jnp.ones((LIN_CHUNK, LIN_CHUNK), bool), -1)
    diff = gc[..., :, None] - gc[..., None, :]
    decay = jnp.where(causal, jnp.exp(jnp.where(causal, diff, 0.0)), 0.0)
    k_beta = k * beta[..., None]
    lower = jnp.where(strict, jnp.einsum('bhncd,bhnsd->bhncs', k_beta, k) * decay, 0.0)
    rhs = jnp.concatenate([v * beta[..., None], k_beta * jnp.exp(gc)[..., None]], axis=-1)
    sol = lax.linalg.triangular_solve(lower + jnp.eye(LIN_CHUNK, dtype=jnp.float32), rhs,
                                      left_side=True, lower=True, unit_diagonal=True)
    u, w = sol[..., :dv], sol[..., dv:]
    attn = jnp.einsum('bhncd,bhnsd->bhncs', q, k) * decay
    q_dec = q * jnp.exp(gc)[..., None]
    g_last = gc[..., -1]
    k_dec = k * jnp.exp(g_last[..., None] - gc)[..., None]

    def step(state, inp):
        q_n, k_n, u_n, w_n, a_n, gl = inp
        v_new = u_n - jnp.einsum('bhcd,bhdv->bhcv', w_n, state)
        o = jnp.einsum('bhcd,bhdv->bhcv', q_n, state) + jnp.einsum('bhcs,bhsv->bhcv', a_n, v_new)
        state = state * jnp.exp(gl)[..., None, None] + jnp.einsum('bhcd,bhcv->bhdv', k_n, v_new)
        return state, o

    xs = tuple(jnp.moveaxis(t, 2, 0) for t in (q_dec, k_dec, u, w, attn, g_last))
    _, o = lax.scan(step, jnp.zeros((bsz, n_heads, dk, dv), jnp.float32), xs)
    return o.transpose(1, 0, 3, 2, 4).reshape(bsz, seq, n_heads, dv)


def moba_attention(q, k, v):
    bsz, seq, n_heads, hd = q.shape
    blk_len, q_len = MOBA_BLOCK, MOBA_Q_CHUNK
    n_blk = -(-seq // blk_len)
    s_pad = n_blk * blk_len
    pad = ((0, 0), (0, s_pad - seq), (0, 0), (0, 0))
    q, k, v = (jnp.pad(t, pad).transpose(0, 2, 1, 3) for t in (q, k, v))
    scale = hd ** -0.5
    slopes = alibi_slopes(n_heads)
    k_blocks = k.reshape(bsz, n_heads, n_blk, blk_len, hd)
    v_blocks = v.reshape(bsz, n_heads, n_blk, blk_len, hd)
    k_mean = jnp.mean(k_blocks.astype(jnp.float32), axis=3)
    gate = jnp.einsum('bhsd,bhnd->bhsn', q.astype(jnp.float32), k_mean)
    q_blk = jnp.arange(s_pad) // blk_len
    gate = jnp.where(jnp.arange(n_blk)[None, :] < q_blk[:, None], gate, -jnp.inf)
    n_sel = min(MOBA_TOPK, n_blk)
    _, sel = lax.top_k(gate, n_sel)
    n_qc = s_pad // q_len
    q_c = q.reshape(bsz, n_heads, n_qc, q_len, hd).transpose(2, 0, 1, 3, 4)
    sel_c = sel.reshape(bsz, n_heads, n_qc, q_len, n_sel).transpose(2, 0, 1, 3, 4)
    b_idx = jnp.arange(bsz)[:, None, None, None]
    h_idx = jnp.arange(n_heads)[None, :, None, None]
    offs = jnp.arange(blk_len)

    def attend(args):
        qc, sc, c = args
        t = c * q_len + jnp.arange(q_len)
        own = (c * q_len) // blk_len
        k_sel = k_blocks[b_idx, h_idx, sc]
        v_sel = v_blocks[b_idx, h_idx, sc]
        s_sel = jnp.einsum('bhqd,bhqnld->bhqnl', qc, k_sel).astype(jnp.float32) * scale
        dist_sel = (t[:, None, None] - (sc[..., None] * blk_len + offs)).astype(jnp.float32)
        s_sel = s_sel - slopes[:, None, None, None] * dist_sel
        s_sel = jnp.where((jnp.arange(n_sel) < own)[:, None], s_sel, -jnp.inf)
        k_own = lax.dynamic_slice_in_dim(k, own * blk_len, blk_len, axis=2)
        v_own = lax.dynamic_slice_in_dim(v, own * blk_len, blk_len, axis=2)
        s_own = jnp.einsum('bhqd,bhld->bhql', qc, k_own).astype(jnp.float32) * scale
        dist_own = t[:, None] - (own * blk_len + offs)[None, :]
        s_own = jnp.where(dist_own >= 0, s_own - slopes[:, None, None] * dist_own.astype(jnp.float32), -jnp.inf)
        scores = jnp.concatenate([s_sel.reshape(bsz, n_heads, q_len, n_sel * blk_len), s_own], axis=-1)
        p = jax.nn.softmax(scores, axis=-1).astype(v.dtype)
        p_sel = p[..., :n_sel * blk_len].reshape(bsz, n_heads, q_len, n_sel, blk_len)
        return (jnp.einsum('bhqnl,bhqnld->bhqd', p_sel, v_sel)
                + jnp.einsum('bhql,bhld->bhqd', p[..., n_sel * blk_len:], v_own))

    o = lax.map(attend, (q_c, sel_c, jnp.arange(n_qc)))
    return o.transpose(1, 0, 3, 2, 4).reshape(bsz, s_pad, n_heads, hd)[:, :seq]


def complex_affine_combine(e1, e2):
    a1r, a1i, b1r, b1i = e1
    a2r, a2i, b2r, b2i = e2
    return (a2r * a1r - a2i * a1i, a2r * a1i + a2i * a1r,
            a2r * b1r - a2i * b1i + b2r, a2r * b1i + a2i * b1r + b2i)


def s5_ssm(u, a_re, a_im, b_re, b_im, c_re, c_im, d, log_step):
    a_re, a_im, b_re, b_im, c_re, c_im, d = (t.astype(jnp.float32) for t in (a_re, a_im, b_re, b_im, c_re, c_im, d))
    step = jnp.exp(log_step.astype(jnp.float32))[:, None]
    mag = jnp.exp(a_re * step)
    ab_re = mag * jnp.cos(a_im * step)
    ab_im = mag * jnp.sin(a_im * step)
    den = jnp.square(a_re) + jnp.square(a_im)
    z_re = ((ab_re - 1.0) * a_re + ab_im * a_im) / den
    z_im = (ab_im * a_re - (ab_re - 1.0) * a_im) / den
    bb_re = z_re[..., None] * b_re - z_im[..., None] * b_im
    bb_im = z_re[..., None] * b_im + z_im[..., None] * b_re
    bu_re = jnp.einsum('gpc,bsgc->bsgp', bb_re, u)
    bu_im = jnp.einsum('gpc,bsgc->bsgp', bb_im, u)
    elems = (jnp.broadcast_to(ab_re, bu_re.shape), jnp.broadcast_to(ab_im, bu_im.shape), bu_re, bu_im)
    _, _, x_re, x_im = lax.associative_scan(complex_affine_combine, elems, axis=1)
    return (jnp.einsum('gcp,bsgp->bsgc', c_re, x_re) - jnp.einsum('gcp,bsgp->bsgc', c_im, x_im) + d * u)


def even_mixer(x, w_in, w_gate2, b_gate, norm_g, w_out):
    bsz, seq, _ = x.shape
    gq, gk, gv, g_lr, gr, mq, mk, mv = split_cols(x @ w_in, EVEN_SIZES)
    kshape = (bsz, seq, GLA_HEADS, GLA_DK)
    vshape = (bsz, seq, GLA_HEADS, GLA_DV)
    log_a = jax.nn.log_sigmoid((g_lr @ w_gate2 + b_gate).astype(jnp.float32)).reshape(kshape) / GLA_GATE_NORM
    o = gla_chunked(gq.reshape(kshape).astype(jnp.float32) * GLA_DK ** -0.5, gk.reshape(kshape).astype(jnp.float32),
                    gv.reshape(vshape).astype(jnp.float32), log_a)
    o = rms_norm(o, norm_g) * jax.nn.silu(gr.reshape(vshape).astype(jnp.float32))
    gla_out = o.reshape(bsz, seq, GLA_HEADS * GLA_DV).astype(x.dtype)
    mshape = (bsz, seq, MOBA_HEADS, MOBA_HEAD_DIM)
    moba_out = moba_attention(mq.reshape(mshape), mk.reshape(mshape), mv.reshape(mshape))
    moba_out = moba_out.reshape(bsz, seq, MOBA_HEADS * MOBA_HEAD_DIM).astype(x.dtype)
    return jnp.concatenate([gla_out, moba_out], axis=-1) @ w_out


def odd_mixer(x, w_in, conv_w, a_log, dt_bias, norm_g, a_re, a_im, b_re, b_im, c_re, c_im, d, log_step,
              glu_w, glu_b, w_out):
    bsz, seq, _ = x.shape
    qkv, z, beta_in, decay_in, u = split_cols(x @ w_in, ODD_SIZES)
    qkv = jax.nn.silu(causal_depthwise_conv(qkv, conv_w))
    q, k, v = jnp.split(qkv, 3, axis=-1)
    hs = (bsz, seq, GDN_HEADS, GDN_HEAD_DIM)
    q = l2_normalize(q.reshape(hs).astype(jnp.float32))
    k = l2_normalize(k.reshape(hs).astype(jnp.float32))
    v = v.reshape(hs).astype(jnp.float32)
    beta = jax.nn.sigmoid(beta_in.astype(jnp.float32))
    g = -jnp.exp(a_log.astype(jnp.float32)) * jax.nn.softplus(decay_in.astype(jnp.float32) + dt_bias.astype(jnp.float32))
    o = gated_delta_chunked(q, k, v, g, beta)
    o = rms_norm(o, norm_g) * jax.nn.silu(z.reshape(hs).astype(jnp.float32))
    gdn_out = o.reshape(bsz, seq, GDN_HEADS * GDN_HEAD_DIM).astype(x.dtype)
    y = s5_ssm(u.reshape(bsz, seq, S5_GROUPS, S5_GROUP_CH).astype(jnp.float32),
               a_re, a_im, b_re, b_im, c_re, c_im, d, log_step)
    y = jax.nn.gelu(y.reshape(bsz, seq, S5_CHANNELS)).astype(x.dtype)
    s5_out = y * jax.nn.sigmoid(y @ glu_w + glu_b)
    return jnp.concatenate([gdn_out, s5_out], axis=-1) @ w_out


def conv_ffn(x, w_up, conv_w, w_down):
    h = causal_depthwise_conv(x @ w_up, conv_w)
    gate, val = jnp.split(h, 2, axis=-1)
    return (jax.nn.silu(gate) * val) @ w_down


def setup_inputs(seed: int = 0) -> dict:
    key = jax.random.key(seed)
    keys = iter(jax.random.split(key, 40))

    def normal(shape, scale):
        return jax.random.normal(next(keys), shape, jnp.float32) * scale

    def uniform(shape, lo, hi):
        return jax.random.uniform(next(keys), shape, jnp.float32, lo, hi)

    dt = jnp.exp(uniform((N_ODD, GDN_HEADS), math.log(1e-3), math.log(1e-1)))
    return {
        "x": normal((BATCH, SEQ, D_MODEL), 1.0),
        "even_w_in": normal((N_EVEN, D_MODEL, EVEN_IN), D_MODEL ** -0.5),
        "gla_w_gate2": normal((N_EVEN, GLA_GATE_RANK, GLA_HEADS * GLA_DK), GLA_GATE_RANK ** -0.5),
        "gla_b_gate": normal((N_EVEN, GLA_HEADS * GLA_DK), 0.02),
        "gla_norm_g": 1.0 + normal((N_EVEN, GLA_DV), 0.02),
        "even_w_out": normal((N_EVEN, EVEN_OUT, D_MODEL), DEEPNORM_BETA * EVEN_OUT ** -0.5),
        "odd_w_in": normal((N_ODD, D_MODEL, ODD_IN), D_MODEL ** -0.5),
        "gdn_conv_w": normal((N_ODD, GDN_CONV, 3 * GDN_HEADS * GDN_HEAD_DIM), GDN_CONV ** -0.5),
        "gdn_a_log": jnp.log(uniform((N_ODD, GDN_HEADS), 1.0, 16.0)),
        "gdn_dt_bias": dt + jnp.log(-jnp.expm1(-dt)),
        "gdn_norm_g": 1.0 + normal((N_ODD, GDN_HEAD_DIM), 0.02),
        "s5_a_re": -0.5 + normal((N_ODD, S5_GROUPS, S5_STATE), 0.01),
        "s5_a_im": math.pi * jnp.arange(S5_STATE, dtype=jnp.float32) + normal((N_ODD, S5_GROUPS, S5_STATE), 0.01),
        "s5_b_re": normal((N_ODD, S5_GROUPS, S5_STATE, S5_GROUP_CH), (2 * S5_GROUP_CH) ** -0.5),
        "s5_b_im": normal((N_ODD, S5_GROUPS, S5_STATE, S5_GROUP_CH), (2 * S5_GROUP_CH) ** -0.5),
        "s5_c_re": normal((N_ODD, S5_GROUPS, S5_GROUP_CH, S5_STATE), S5_STATE ** -0.25),
        "s5_c_im": normal((N_ODD, S5_GROUPS, S5_GROUP_CH, S5_STATE), S5_STATE ** -0.25),
        "s5_d": normal((N_ODD, S5_GROUPS, S5_GROUP_CH), 0.5),
        "s5_log_step": uniform((N_ODD, S5_GROUPS), math.log(1e-3), math.log(1e-1)),
        "s5_glu_w": normal((N_ODD, S5_CHANNELS, S5_CHANNELS), S5_CHANNELS ** -0.5),
        "s5_glu_b": normal((N_ODD, S5_CHANNELS), 0.02),
        "odd_w_out": normal((N_ODD, ODD_OUT, D_MODEL), DEEPNORM_BETA * ODD_OUT ** -0.5),
        "ln_mix_g": 1.0 + normal((DEPTH, D_MODEL), 0.02),
        "ln_mix_b": normal((DEPTH, D_MODEL), 0.02),
        "ffn_w_up": normal((DEPTH, D_MODEL, 2 * D_FF), D_MODEL ** -0.5),
        "ffn_conv_w": normal((DEPTH, FFN_CONV, 2 * D_FF), FFN_CONV ** -0.5),
        "ffn_w_down": normal((DEPTH, D_FF, D_MODEL), DEEPNORM_BETA * D_FF ** -0.5),
        "ln_ffn_g": 1.0 + normal((DEPTH, D_MODEL), 0.02),
        "ln_ffn_b": normal((DEPTH, D_MODEL), 0.02),
    }


def reference(x, even_w_in, gla_w_gate2, gla_b_gate, gla_norm_g, even_w_out, odd_w_in, gdn_conv_w, gdn_a_log,
              gdn_dt_bias, gdn_norm_g, s5_a_re, s5_a_im, s5_b_re, s5_b_im, s5_c_re, s5_c_im, s5_d, s5_log_step,
              s5_glu_w, s5_glu_b, odd_w_out, ln_mix_g, ln_mix_b, ffn_w_up, ffn_conv_w, ffn_w_down, ln_ffn_g, ln_ffn_b):
    for i in range(DEPTH):
        j = i // 2
        if i % 2 == 0:
            mix = even_mixer(x, even_w_in[j], gla_w_gate2[j], gla_b_gate[j], gla_norm_g[j], even_w_out[j])
        else:
            mix = odd_mixer(x, odd_w_in[j], gdn_conv_w[j], gdn_a_log[j], gdn_dt_bias[j], gdn_norm_g[j],
                            s5_a_re[j], s5_a_im[j], s5_b_re[j], s5_b_im[j], s5_c_re[j], s5_c_im[j], s5_d[j],
                            s5_log_step[j], s5_glu_w[j], s5_glu_b[j], odd_w_out[j])
        x = layer_norm(DEEPNORM_ALPHA * x + mix, ln_mix_g[i], ln_mix_b[i])
        x = layer_norm(DEEPNORM_ALPHA * x + conv_ffn(x, ffn_w_up[i], ffn_conv_w[i], ffn_w_down[i]),
                       ln_ffn_g[i], ln_ffn_b[i])
    return x
```

```python
import contextlib
import numpy as np
import concourse.bass as bass
import concourse.mybir as mybir

F32 = mybir.dt.float32
BF16 = mybir.dt.bfloat16
I32 = mybir.dt.int32
AF = mybir.ActivationFunctionType
ALU = mybir.AluOpType
AX = mybir.AxisListType

ENGS = ["tensor", "vector", "scalar", "gpsimd", "sync"]


class Dep:
    __slots__ = ("w", "r", "dsem", "dcnt", "name", "psum")

    def __init__(self, name=""):
        self.w = None
        self.r = {}
        self.dsem = None
        self.dcnt = 0
        self.name = name
        self.psum = False


class T:
    def __init__(self, h, name, nsub=0):
        self.h = h
        self.dep = Dep(name)
        self.name = name
        self.s = [Dep(f"{name}.{i}") for i in range(nsub)]

    def __getitem__(self, idx):
        return self.h[idx]

    def ap(self):
        return self.h.ap() if hasattr(self.h, "ap") else self.h[:]


class Prog:
    def __init__(self, nc, same_engine_sync=True):
        self.nc = nc
        self.stack = contextlib.ExitStack()
        self.ops = {e: [] for e in ENGS}
        self.cnt = {e: 0 for e in ENGS}
        self.known = {e: {} for e in ENGS}
        self.sems = {}
        self.same = same_engine_sync
        self.n_dsem = 0
        self.uid = 0
        self.total = 0
        self.max_ops = None
        self.marks = []
        for e in ENGS:
            self._sem("E_" + e)

    def _sem(self, key):
        if key not in self.sems:
            self.sems[key] = self.stack.enter_context(self.nc.semaphore(key))
        return key

    def sb(self, name, shape, dtype=F32, nsub=0):
        self.uid += 1
        h = self.stack.enter_context(self.nc.sbuf_tensor(f"{name}_{self.uid}", list(shape), dtype))
        return T(h, name, nsub)

    def ps(self, name, shape, dtype=F32):
        self.uid += 1
        h = self.stack.enter_context(self.nc.psum_tensor(f"{name}_{self.uid}", list(shape), dtype))
        t = T(h, name)
        t.dep.psum = True
        return t

    def dram(self, name, shape, dtype=F32, kind="Internal"):
        h = self.nc.dram_tensor(name, list(shape), dtype, kind=kind)
        return T(h, name)

    @staticmethod
    def _deps(x):
        out = []
        for t in x:
            if t is None:
                continue
            if isinstance(t, (list, tuple)):
                out.extend(Prog._deps(t))
            elif isinstance(t, T):
                out.append(t.dep)
                out.extend(t.s)
            else:
                out.append(t)
        return out

    def _collect(self, eng, reads, writes):
        need = {}
        for d in reads:
            if d.w is not None:
                k, v = d.w
                need[k] = max(need.get(k, 0), v)
            if d.psum:
                for k, v in d.r.items():
                    if k != "E_" + eng:
                        need[k] = max(need.get(k, 0), v)
        for d in writes:
            if d.w is not None:
                k, v = d.w
                need[k] = max(need.get(k, 0), v)
            for k, v in d.r.items():
                need[k] = max(need.get(k, 0), v)
        waits = []
        own = "E_" + eng
        kn = self.known[eng]
        for k, v in need.items():
            if k == own and (eng == "tensor" or not self.same):
                continue
            if kn.get(k, 0) >= v:
                continue
            kn[k] = v
            waits.append((k, v))
        return waits

    def _commit(self, ev, reads, writes):
        k, v = ev
        for d in reads:
            d.r[k] = max(d.r.get(k, 0), v)
        for d in writes:
            d.w = ev
            d.r = {}

    def mark(self, name):
        self.marks.append((name, self.total))

    def _skip(self):
        self.total += 1
        return self.max_ops is not None and self.total > self.max_ops

    def op(self, eng, fn, reads=(), writes=()):
        if self._skip():
            return
        reads = self._deps(reads)
        writes = self._deps(writes)
        waits = self._collect(eng, reads, writes)
        self.cnt[eng] += 1
        ev = ("E_" + eng, self.cnt[eng])
        self.known[eng]["E_" + eng] = max(self.known[eng].get("E_" + eng, 0), 0)
        self.ops[eng].append((waits, fn, ev[0], 1))
        self._commit(ev, reads, writes)

    def dma(self, queue, out, in_, reads=(), writes=(), sem_dep=None, **kw):
        if self._skip():
            return None
        reads = self._deps(reads)
        writes = self._deps(writes)
        sd = self._deps([sem_dep])[0] if sem_dep is not None else (writes[0] if writes else reads[0])
        if sd.dsem is None:
            self.n_dsem += 1
            sd.dsem = self._sem(f"D{self.n_dsem}_{sd.name}"[:30])
        waits = self._collect(queue, reads, writes)
        sd.dcnt += 1
        ev = (sd.dsem, 16 * sd.dcnt)
        fn = (lambda e, out=out, in_=in_, kw=kw: e.dma_start(out=out, in_=in_, **kw))
        self.ops[queue].append((waits, fn, ev[0], 16))
        self._commit(ev, reads, writes)
        return ev

    def wait_event(self, eng, ev):
        if ev is None:
            return
        k, v = ev
        self.ops[eng].append(([(k, v)], None, None, 0))

    def mm(self, out_ap, lhsT_ap, rhs_ap, start, stop, reads, writes, **kw):
        self.op("tensor", lambda e: e.matmul(out_ap, lhsT_ap, rhs_ap, start=start, stop=stop, **kw),
                reads, writes)

    def tr(self, out_ap, in_ap, ident_ap, reads, writes):
        self.op("tensor", lambda e: e.transpose(out_ap, in_ap, ident_ap), reads, writes)

    def act(self, out_ap, in_ap, func, reads, writes, eng="scalar", **kw):
        self.op(eng, lambda e: e.activation(out=out_ap, in_=in_ap, func=func, **kw), reads, writes)

    def tt(self, eng, out_ap, in0, in1, op, reads, writes):
        self.op(eng, lambda e: e.tensor_tensor(out=out_ap, in0=in0, in1=in1, op=op), reads, writes)

    def ts(self, eng, out_ap, in0, s1, s2, op0, op1, reads, writes):
        if op1 is None:
            self.op(eng, lambda e: e.tensor_scalar(out=out_ap, in0=in0, scalar1=s1, scalar2=None, op0=op0),
                    reads, writes)
        else:
            self.op(eng, lambda e: e.tensor_scalar(out=out_ap, in0=in0, scalar1=s1, scalar2=s2, op0=op0, op1=op1),
                    reads, writes)

    def stt(self, out_ap, in0, scalar, in1, op0, op1, reads, writes, eng="vector"):
        self.op(eng, lambda e: e.scalar_tensor_tensor(out=out_ap, in0=in0, scalar=scalar, in1=in1,
                                                      op0=op0, op1=op1), reads, writes)

    def copy(self, eng, out_ap, in_ap, reads, writes):
        if eng == "scalar":
            self.op(eng, lambda e: e.copy(out=out_ap, in_=in_ap), reads, writes)
        else:
            self.op(eng, lambda e: e.tensor_copy(out=out_ap, in_=in_ap), reads, writes)

    def memset(self, eng, ap, val, writes):
        self.op(eng, lambda e: e.memset(ap, val), (), writes)

    def emit(self):
        nc = self.nc
        prog = self
        with nc.Block() as block:
            def make(engname):
                def body(eng):
                    for waits, fn, semk, inc in prog.ops[engname]:
                        for k, v in waits:
                            eng.wait_ge(prog.sems[k], v)
                        if fn is not None:
                            inst = fn(eng)
                            inst.then_inc(prog.sems[semk], inc)
                return body
            for e in ENGS:
                if prog.ops[e]:
                    getattr(block, e)(make(e))
        self.stack.close()

    def stats(self):
        return {e: len(self.ops[e]) for e in ENGS}, len(self.sems)


D = 2048
KC = 16
DFF = 5632
FC = 44
NTOK = 1024
NH = 1026
ALPHA = 8 ** 0.25
LN_EPS = 1e-5
TILES3 = [(0, 342), (342, 342), (684, 342)]
UPT = [(0, 344), (342, 344), (684, 342)]


class Banks:
    def __init__(self, P, n=8):
        self.b = [P.ps(f"bank{i}", [128, 512], F32) for i in range(n)]
        self.i = 0

    def get(self):
        t = self.b[self.i % len(self.b)]
        self.i += 1
        return t


class Rot:
    def __init__(self, items):
        self.items = items
        self.i = 0

    def get(self):
        t = self.items[self.i % len(self.items)]
        self.i += 1
        return t


def layer_norm_fm(P, banks, y, ncols, tiles, g, b, gbdep, ones_f, scr, outs):
    for (t0, n) in tiles:
        ps1 = banks.get()
        ps2 = banks.get()
        for kc in range(KC):
            P.mm(ps1[:, 0:n], ones_f[:], y[:, kc, t0:t0 + n], kc == 0, kc == KC - 1, [ones_f, y.s[kc]], [ps1])
        for kc in range(KC):
            sq = scr["sq"].get()
            P.act(sq[:, 0:n], y[:, kc, t0:t0 + n], AF.Square, [y.s[kc]], [sq])
            P.mm(ps2[:, 0:n], ones_f[:], sq[:, 0:n], kc == 0, kc == KC - 1, [ones_f, sq], [ps2])
        mean = scr["mean"]
        msq = scr["msq"]
        rstd = scr["rstd"]
        nmr = scr["nmr"]
        P.ts("vector", mean[:, 0:n], ps1[:, 0:n], 1.0 / D, None, ALU.mult, None, [ps1], [mean])
        P.tt("vector", msq[:, 0:n], mean[:, 0:n], mean[:, 0:n], ALU.mult, [mean], [msq])
        P.stt(rstd[:, 0:n], ps2[:, 0:n], 1.0 / D, msq[:, 0:n], ALU.mult, ALU.subtract, [ps2, msq], [rstd])
        P.ts("vector", rstd[:, 0:n], rstd[:, 0:n], LN_EPS, None, ALU.add, None, [rstd], [rstd])
        P.act(rstd[:, 0:n], rstd[:, 0:n], AF.Sqrt, [rstd], [rstd])
        P.op("vector", lambda e, n=n: e.reciprocal(out=rstd[:, 0:n], in_=rstd[:, 0:n]), [rstd], [rstd])
        P.stt(nmr[:, 0:n], mean[:, 0:n], -1.0, rstd[:, 0:n], ALU.mult, ALU.mult, [mean, rstd], [nmr])
        for kc in range(KC):
            tmp = scr["tmp"].get()
            P.tt("vector", tmp[:, 0:n], y[:, kc, t0:t0 + n], rstd[:, 0:n], ALU.mult, [y.s[kc], rstd], [tmp])
            P.tt("gpsimd", tmp[:, 0:n], tmp[:, 0:n], nmr[:, 0:n], ALU.add, [tmp, nmr], [tmp])
            for (o, off, lo) in outs:
                s = max(t0, lo)
                if s >= t0 + n:
                    continue
                P.act(o[:, kc, s - lo + off:t0 + n - lo + off], tmp[:, s - t0:n], AF.Identity, [tmp, gbdep], [o.s[kc]],
                      scale=g(kc), bias=b(kc))


def build_post(P, glu=False):
    oT = P.dram("oT", [D, NH], BF16, kind="ExternalInput")
    xT = P.dram("xT", [D, NH], F32, kind="ExternalInput")
    flag = P.dram("flag", [128, 1], F32, kind="ExternalInput")
    w_out = P.dram("w_out", [D, D], F32, kind="ExternalInput")
    lnp = P.dram("lnp", [128, 4, KC], F32, kind="ExternalInput")
    w_up = P.dram("w_up", [D, 2 * DFF], F32, kind="ExternalInput")
    cw = P.dram("cw", [128, 3, 2 * FC], F32, kind="ExternalInput")
    w_down = P.dram("w_down", [DFF, D], F32, kind="ExternalInput")
    if glu:
        glu_w = P.dram("glu_w", [1024, 1024], F32, kind="ExternalInput")
        glu_b = P.dram("glu_b", [128, 8], F32, kind="ExternalInput")
    xo = P.dram("xo", [D, NTOK], F32, kind="ExternalOutput")
    xob = P.dram("xob", [D, NTOK], BF16, kind="ExternalOutput")

    banks = Banks(P)
    xres = P.sb("xres", [128, KC, NH], F32, nsub=KC)
    xmb = P.sb("xmb", [128, KC, NH], BF16, nsub=KC)
    lnp_s = P.sb("lnp_s", [128, 4, KC], F32)
    cw_s = P.sb("cw_s", [128, 3, 2 * FC], F32)
    flag_s = P.sb("flag_s", [128, 1], F32)
    ones_f = P.sb("ones_f", [128, 128], F32)
    P.memset("vector", ones_f[:], 1.0, [ones_f])
    P.dma("sync", lnp_s[:], lnp.ap(), [], [lnp_s])
    P.dma("sync", cw_s[:], cw.ap(), [], [cw_s])
    P.dma("sync", flag_s[:], flag.ap(), [], [flag_s])
    xT_v = xT.ap().rearrange("(kc p) t -> p kc t", p=128)
    for kc in range(KC):
        P.dma("sync", xres[:, kc, :], xT_v[:, kc, :], [], [xres.s[kc]])

    stg = Rot([P.sb(f"stg{i}", [128, KC, 128], F32) for i in range(2)])
    wbf = Rot([P.sb(f"wbf{i}", [128, KC, 128], BF16) for i in range(3)])
    scr = {
        "sq": Rot([P.sb(f"sq{i}", [128, 344], F32) for i in range(2)]),
        "tmp": Rot([P.sb(f"lt{i}", [128, 344], F32) for i in range(2)]),
        "mean": P.sb("mean", [128, 344], F32),
        "msq": P.sb("msq", [128, 344], F32),
        "rstd": P.sb("rstd", [128, 344], F32),
        "nmr": P.sb("nmr", [128, 344], F32),
    }

    def load_w(dram_t, col0):
        s = stg.get()
        v = dram_t.ap().rearrange("(kc p) c -> p kc c", p=128)
        P.dma("sync", s[:], v[:, :, col0:col0 + 128], [], [s])
        wb = wbf.get()
        P.copy("gpsimd", wb[:], s[:], [s], [wb])
        return wb

    oT_v = oT.ap().rearrange("(kc p) t -> p kc t", p=128)
    for kc in range(KC):
        P.dma("sync", xmb[:, kc, :], oT_v[:, kc, :], [], [xmb.s[kc]])
    if glu:
        glb_s = P.sb("glb_s", [128, 8], F32)
        P.dma("sync", glb_s[:], glu_b.ap(), [], [glb_s])
        s5t = P.sb("s5t", [128, 8, 342], BF16)
        sgt = Rot([P.sb(f"sgt{i}", [128, 342], F32) for i in range(2)])
        gv = glu_w.ap().rearrange("(kc p) c -> p kc c", p=128)
        for (t0, n) in TILES3:
            for j in range(8):
                s_ = stg.get()
                P.dma("sync", s_[:, 0:8, :], gv[:, :, j * 128:(j + 1) * 128], [], [s_])
                wb = wbf.get()
                P.copy("gpsimd", wb[:, 0:8, :], s_[:, 0:8, :], [s_], [wb])
                ps = banks.get()
                for kc in range(8):
                    P.mm(ps[:, 0:n], wb[:, kc, :], xmb[:, 8 + kc, t0:t0 + n], kc == 0, kc == 7,
                         [wb, xmb.s[8 + kc]], [ps])
                sg_ = sgt.get()
                P.act(sg_[:, 0:n], ps[:, 0:n], AF.Sigmoid, [ps, glb_s], [sg_], bias=glb_s[:, j:j + 1])
                P.tt("vector", s5t[:, j, 0:n], sg_[:, 0:n], xmb[:, 8 + j, t0:t0 + n], ALU.mult,
                     [sg_, xmb.s[8 + j]], [s5t])
            for j in range(8):
                P.copy("gpsimd" if j % 2 else "vector", xmb[:, 8 + j, t0:t0 + n], s5t[:, j, 0:n],
                       [s5t], [xmb.s[8 + j]])
    for j in range(KC):
        wb = load_w(w_out, j * 128)
        for (t0, n) in TILES3:
            ps = banks.get()
            for kc in range(KC):
                P.mm(ps[:, 0:n], wb[:, kc, :], xmb[:, kc, t0:t0 + n], kc == 0, kc == KC - 1, [wb, xmb.s[kc]], [ps])
            P.stt(xres[:, j, t0:t0 + n], xres[:, j, t0:t0 + n], ALPHA, ps[:, 0:n], ALU.mult, ALU.add,
                  [xres.s[j], ps], [xres.s[j]])
    def V(row):
        return lambda kc: lnp_s[:, row, kc:kc + 1]
    layer_norm_fm(P, banks, xres, NH, TILES3, V(0), V(1), lnp_s, ones_f, scr, [(xres, 0, 0)])
    P.ts("vector", xres[:, :, 0:2], xres[:, :, 0:2], flag_s[:, 0:1], None, ALU.mult, None, [xres, flag_s], [xres])
    for kc in range(KC):
        P.copy("gpsimd" if kc % 2 else "vector", xmb[:, kc, :], xres[:, kc, :], [xres.s[kc]], [xmb.s[kc]])
    for kc in range(KC):
        P.ts("vector", xres[:, kc, :], xres[:, kc, :], ALPHA, None, ALU.mult, None,
             [xres.s[kc]], [xres.s[kc]])

    GRP = 4
    NG = FC // GRP
    hT = Rot([P.sb(f"hT{i}", [128, GRP, NTOK], BF16) for i in range(2)])
    ctmp = Rot([P.sb(f"ct{i}", [128, NTOK], F32) for i in range(3)])
    dstg = Rot([P.sb(f"dstg{i}", [128, GRP, 512], F32) for i in range(2)])
    dbf = Rot([P.sb(f"dbf{i}", [128, GRP, 512], BF16) for i in range(2)])
    w_down_v = w_down.ap().rearrange("(fc p) c -> p fc c", p=128)

    def conv_from_psum(pss, cidx, eng_first="scalar"):
        ct = ctmp.get()
        for (ps, (t0, n)) in zip(pss, UPT):
            m = n - 2
            o = ct[:, t0:t0 + m]
            P.ts("vector", o, ps[:, 0:m], cw_s[:, 0, cidx:cidx + 1], None, ALU.mult, None, [ps, cw_s], [ct])
            P.stt(o, ps[:, 1:m + 1], cw_s[:, 1, cidx:cidx + 1], o, ALU.mult, ALU.add, [ps, ct, cw_s], [ct])
            P.stt(o, ps[:, 2:m + 2], cw_s[:, 2, cidx:cidx + 1], o, ALU.mult, ALU.add, [ps, ct, cw_s], [ct])
        return ct

    for g in range(NG):
        h = hT.get()
        for ci in range(GRP):
            c = g * GRP + ci
            res = []
            for half, col0 in ((0, c * 128), (1, DFF + c * 128)):
                wb = load_w(w_up, col0)
                pss = []
                for (t0, n) in UPT:
                    ps = banks.get()
                    for kc in range(KC):
                        P.mm(ps[:, 0:n], wb[:, kc, :], xmb[:, kc, t0:t0 + n], kc == 0, kc == KC - 1,
                             [wb, xmb.s[kc]], [ps])
                    pss.append(ps)
                res.append(conv_from_psum(pss, half * FC + c))
            gt, vt = res
            P.act(gt[:], gt[:], AF.Silu, [gt], [gt])
            P.tt("gpsimd", h[:, ci, :], gt[:], vt[:], ALU.mult, [gt, vt], [h])
        for cb in range(4):
            s = dstg.get()
            P.dma("sync", s[:], w_down_v[:, g * GRP:(g + 1) * GRP, cb * 512:(cb + 1) * 512], [], [s])
            wd = dbf.get()
            P.copy("gpsimd", wd[:], s[:], [s], [wd])
            for jj in range(4):
                j = cb * 4 + jj
                for tt_ in range(2):
                    ps = banks.get()
                    for ci in range(GRP):
                        P.mm(ps[:, :], wd[:, ci, jj * 128:(jj + 1) * 128], h[:, ci, tt_ * 512:(tt_ + 1) * 512],
                             ci == 0, ci == GRP - 1, [wd, h], [ps])
                    dst = xres[:, j, 2 + tt_ * 512:2 + (tt_ + 1) * 512]
                    P.tt("vector", dst, dst, ps[:, :], ALU.add, [xres.s[j], ps], [xres.s[j]])
    T2 = [(2, 342), (344, 341), (685, 341)]
    layer_norm_fm(P, banks, xres, NH, T2, V(2), V(3), lnp_s, ones_f, scr, [(xres, 0, 0)])
    xo_v = xo.ap().rearrange("(kc p) t -> p kc t", p=128)
    xob_v = xob.ap().rearrange("(kc p) t -> p kc t", p=128)
    evs = []
    for kc in range(KC):
        P.copy("gpsimd" if kc % 2 else "vector", xmb[:, kc, 2:NH], xres[:, kc, 2:NH], [xres.s[kc]], [xmb.s[kc]])
    for kc in range(KC):
        evs.append(P.dma("sync", xo_v[:, kc, :], xres[:, kc, 2:NH], [xres.s[kc]], []))
    for kc in range(KC):
        evs.append(P.dma("sync", xob_v[:, kc, :], xmb[:, kc, 2:NH], [xmb.s[kc]], []))
    for ev in evs:
        P.wait_event("sync", ev)


S = 8192
TT = 512
NTI = S // TT
EW = 784
RMS_EPS = 1e-6
C_Q, C_K, C_GR, C_MQ, C_MK, C_LR, C_V = 0, 64, 128, 256, 384, 512, 528


def load_x_tile(P, xT, i, x_is_f32, xbufs, xstg, width=TT, eng_cast="gpsimd"):
    xt = xbufs.get()
    v = xT.ap().rearrange("(kc p) t -> p kc t", p=128)
    for q in range(4):
        src = v[:, q * 4:(q + 1) * 4, i * width:(i + 1) * width]
        if x_is_f32:
            s = xstg.get()
            P.dma("sync", s[:, :, 0:width], src, [], [s])
            P.copy(eng_cast, xt[:, q * 4:(q + 1) * 4, 0:width], s[:, :, 0:width], [s], [xt.s[q]])
        else:
            P.dma("sync", xt[:, q * 4:(q + 1) * 4, 0:width], src, [], [xt.s[q]])
    return xt


def load_w_cols(P, wh, w_bf, ncols, stg):
    for kc in range(KC):
        s = stg.get()
        P.dma("sync", s[:, 0:ncols], wh.ap()[kc * 128:(kc + 1) * 128, :], [], [s])
        P.copy("gpsimd", w_bf[:, kc, :], s[:, 0:ncols], [s], [w_bf])


def build_even(P, x_is_f32, ntiles=NTI, do_gla=True, do_moba=True, seq=S):
    xT = P.dram("xT", [D, seq], F32 if x_is_f32 else BF16, kind="ExternalInput")
    wh = P.dram("wh", [D, EW], F32, kind="ExternalInput")
    wg2 = P.dram("wg2", [128, 128], F32, kind="ExternalInput")
    bg = P.dram("bg", [64, 1], F32, kind="ExternalInput")
    ng = P.dram("ng", [128, 1], F32, kind="ExternalInput")
    c_tri = P.dram("c_tri", [128, 128], F32, kind="ExternalInput")
    c_scan = P.dram("c_scan", [64, TT], F32, kind="ExternalInput")
    c_negpad = P.dram("c_negpad", [128, 64], F32, kind="ExternalInput")
    c_esel = P.dram("c_esel", [128, 32 * 128], F32, kind="ExternalInput")
    c_alibi = P.dram("c_alibi", [128, 64], F32, kind="ExternalInput")
    c_ident = P.dram("c_ident", [128, 128], F32, kind="ExternalInput")
    oT = P.dram("oT", [256, seq], BF16, kind="ExternalOutput")

    gen = Banks(P, 3)
    sc_banks = [P.ps(f"scb{i}", [128, 512], F32) for i in range(2)]
    sc_rot = Rot(sc_banks)
    acc_banks = Rot([P.ps(f"acc{i}", [128, 512], F32) for i in range(1)])
    ps_o = P.ps("ps_o", [128, 512], F32)
    trb = P.ps("trb", [128, 1024], BF16)

    tri_f = P.sb("tri_f", [128, 128], F32)
    tri_b = P.sb("tri_b", [128, 128], BF16)
    scanm = P.sb("scanm", [64, TT], F32)
    negpad = P.sb("negpad", [128, 64], F32)
    esel_f = P.sb("esel_f", [128, 32 * 128], F32)
    esel = P.sb("esel", [128, 32 * 128], BF16)
    alibi = P.sb("alibi", [128, 64], F32)
    ident_f = P.sb("ident_f", [128, 128], F32)
    ident_b = P.sb("ident_b", [128, 128], BF16)
    ones_f = P.sb("ones_f", [128, 128], F32)
    wg2_s = P.sb("wg2_s", [128, 128], F32)
    negb = P.sb("negb", [64, 1], F32)
    ng_s = P.sb("ng_s", [128, 1], F32)
    for dst, src in ((tri_f, c_tri), (scanm, c_scan), (negpad, c_negpad), (alibi, c_alibi),
                     (ident_f, c_ident), (negb, bg), (ng_s, ng)):
        P.dma("sync", dst[:], src.ap(), [], [dst])
    P.dma("sync", wg2_s[:], wg2.ap(), [], [wg2_s])
    for q_ in range(4):
        P.dma("sync", esel_f[:, q_ * 1024:(q_ + 1) * 1024], c_esel.ap()[:, q_ * 1024:(q_ + 1) * 1024], [], [esel_f])
    P.copy("vector", tri_b[:], tri_f[:], [tri_f], [tri_b])
    for q_ in range(8):
        P.copy("vector", esel[:, q_ * 512:(q_ + 1) * 512], esel_f[:, q_ * 512:(q_ + 1) * 512], [esel_f], [esel])
    P.copy("vector", ident_b[:], ident_f[:], [ident_f], [ident_b])
    P.memset("vector", ones_f[:], 1.0, [ones_f])
    P.ts("vector", negb[:], negb[:], -1.0, None, ALU.mult, None, [negb], [negb])

    w_bf = P.sb("w_bf", [128, KC, EW], BF16)
    wstg = Rot([P.sb(f"wstg{i}", [128, EW], F32) for i in range(2)])
    load_w_cols(P, wh, w_bf, EW, wstg)

    kT_all = P.sb("kT_all", [128, seq], BF16, nsub=NTI)
    V_all = P.sb("V_all", [128, seq // 128, 144], BF16, nsub=NTI)
    kmean = P.sb("kmean", [128, 32], F32)
    S_f = P.sb("S_f", [64, 128], F32)
    S_b = P.sb("S_b", [128, 128], BF16)
    P.memset("gpsimd", V_all[:], 1.0, [V_all])
    P.memset("vector", kmean[:], 0.0, [kmean])
    P.memset("vector", S_f[:], 0.0, [S_f])
    P.memset("vector", S_b[:], 0.0, [S_b])

    xbufs = Rot([P.sb(f"xt{i}", [128, KC, TT], BF16, nsub=4) for i in range(2)])
    xstg = Rot([P.sb(f"xs{i}", [128, 4, TT], F32) for i in range(2)]) if x_is_f32 else None

    def sbt(name, shape, dt=F32, n=2):
        return Rot([P.sb(f"{name}{i}", shape, dt) for i in range(n)])
    glr_s = sbt("glr", [128, TT]); la = sbt("la", [64, TT]); cum = sbt("cum", [64, TT])
    E1 = sbt("E1", [64, TT]); E2 = sbt("E2", [64, TT])
    q_e = sbt("q_e", [128, TT], BF16); k_e = sbt("k_e", [128, TT], BF16)
    for t_ in q_e.items + k_e.items:
        P.memset("vector", t_[:], 0.0, [t_])
    ke_tok = sbt("ke_tok", [128, 4, 128], BF16)
    v_bf = sbt("v_bf", [128, 4, 128], BF16)
    attn_s = sbt("attn_s", [128, 128], BF16, 3)
    S1 = sbt("S1", [64, 128])
    sq = sbt("sq", [128, TT], F32, 1); rstd = sbt("rstd", [128, TT], F32, 1); t1 = sbt("t1", [128, TT], F32, 1)
    sg = sbt("sg", [128, TT], F32, 1); go = sbt("go", [128, TT], BF16)
    mq_f = sbt("mq_f", [128, TT]); mq_b = sbt("mq_b", [128, TT], BF16)
    gm = sbt("gm", [128, 32]); top8 = sbt("top8", [128, 8]); selb = sbt("selb", [128, 128])
    for t_ in selb.items:
        P.memset("vector", t_[:], 0.0, [t_])
    selbT = sbt("selbT", [128, TT], BF16)
    for t_ in selbT.items:
        P.memset("vector", t_[:], 0.0, [t_])
    pT = sbt("pT", [128, 128], BF16, 12)
    rden = sbt("rden", [128, 1]); o_n = sbt("o_n", [128, 128], BF16)
    mo_b = sbt("mo_b", [128, TT], BF16)
    out_evs = []

    for i in range(ntiles):
        xt = load_x_tile(P, xT, i, x_is_f32, xbufs, xstg)

        def proj_fm(col0, M):
            ps = gen.get()
            for kc in range(KC):
                P.mm(ps[:, :], w_bf[:, kc, col0:col0 + 128], xt[:, kc, :], kc == 0, kc == KC - 1,
                     [w_bf, xt.s[kc // 4]], [ps])
            return ps

        if do_gla:
            P.mark('tile_start')
            ps_lr = proj_fm(C_LR, 128)
            gl = glr_s.get()
            P.copy("vector", gl[:], ps_lr[:, :], [ps_lr], [gl])
            ps_g = gen.get()
            P.mm(ps_g[:, :], wg2_s[:], gl[:], True, True, [wg2_s, gl], [ps_g])
            la_t = la.get()
            P.act(la_t[:], ps_g[0:64, :], AF.Exp, [ps_g, negb], [la_t], scale=-1.0, bias=negb[:, 0:1])
            P.act(la_t[:], la_t[:], AF.Ln, [la_t, ones_f], [la_t], bias=ones_f[0:64, 0:1])
            P.ts("vector", la_t[:], la_t[:], -1.0 / 16.0, None, ALU.mult, None, [la_t], [la_t])
            P.mark('pre_scan')
            cu = cum.get()
            P.op("vector", lambda e, cu=cu, la_t=la_t: e.tensor_tensor_scan(
                out=cu[:], data0=scanm[:], data1=la_t[:], initial=0.0, op0=ALU.mult, op1=ALU.add),
                [scanm, la_t], [cu])
            e1 = E1.get(); e2 = E2.get()
            P.act(e1[:], cu[:], AF.Exp, [cu], [e1])
            P.act(e2[:], cu[:], AF.Exp, [cu], [e2], scale=-1.0)
            P.mark('pre_q')
            ps_q = proj_fm(C_Q, 64)
            qe = q_e.get()
            P.stt(qe[0:64, :], ps_q[0:64, :], 0.125, e1[:], ALU.mult, ALU.mult, [ps_q, e1], [qe])
            ps_k = proj_fm(C_K, 64)
            ke = k_e.get()
            P.tt("vector", ke[0:64, :], ps_k[0:64, :], e2[:], ALU.mult, [ps_k, e2], [ke])
        P.mark('pre_v')
        vb = v_bf.get()
        for tg in range(4):
            ps_v = gen.get()
            for kc in range(KC):
                P.mm(ps_v[:, 0:256], xt[:, kc, tg * 128:(tg + 1) * 128], w_bf[:, kc, C_V:C_V + 256],
                     kc == 0, kc == KC - 1, [w_bf, xt.s[kc // 4]], [ps_v])
            P.copy("vector", vb[:, tg, :], ps_v[:, 0:128], [ps_v], [vb])
            P.copy("vector", V_all[:, 4 * i + tg, 0:128], ps_v[:, 128:256], [ps_v], [V_all.s[i]])
        if do_gla:
            P.mark('pre_tr')
            kt_ = ke_tok.get()
            for j in range(4):
                P.tr(trb[:, j * 128:(j + 1) * 128], ke[:, j * 128:(j + 1) * 128], ident_b[:, :],
                     [ke, ident_b], [trb])
            P.copy("vector", kt_[:, :, :], trb[:, 0:512].rearrange("p (j c) -> p j c", j=4), [trb], [kt_])
            P.mark('pre_chunks')
            for j in range(4):
                cs = slice(j * 128, (j + 1) * 128)
                ps_a = gen.get()
                P.mm(ps_a[:, 0:128], ke[:, cs], qe[:, cs], True, True, [ke, qe], [ps_a])
                at = attn_s.get()
                P.tt("vector", at[:], ps_a[:, 0:128], tri_f[:], ALU.mult, [ps_a, tri_f], [at])
                P.mm(ps_o[:, cs], vb[:, j, :], at[:], True, False, [vb, at], [ps_o])
                P.mm(ps_o[:, cs], S_b[:], qe[:, cs], False, True, [S_b, qe], [ps_o])
                ps_kv = gen.get()
                P.mm(ps_kv[:, 0:128], kt_[:, j, :], vb[:, j, :], True, True, [kt_, vb], [ps_kv])
                el = e1[:, j * 128 + 127:j * 128 + 128]
                s1 = S1.get()
                P.ts("vector", s1[:], S_f[:], el, None, ALU.mult, None, [S_f, e1], [s1])
                P.stt(S_f[:], ps_kv[0:64, 0:128], el, s1[:], ALU.mult, ALU.add, [ps_kv, e1, s1], [S_f])
                P.copy("vector", S_b[0:64, :], S_f[:], [S_f], [S_b])
            P.mark('pre_epi')
            sq_t = sq.get(); rs = rstd.get(); t1_t = t1.get(); sg_t = sg.get(); go_t = go.get()
            P.act(sq_t[:], ps_o[:, :], AF.Square, [ps_o], [sq_t])
            ps_ms = gen.get()
            P.mm(ps_ms[:, :], ones_f[:], sq_t[:], True, True, [ones_f, sq_t], [ps_ms])
            P.ts("vector", rs[:], ps_ms[:, :], 1.0 / 128.0, RMS_EPS, ALU.mult, ALU.add, [ps_ms], [rs])
            P.act(rs[:], rs[:], AF.Sqrt, [rs], [rs])
            P.op("vector", lambda e, rs=rs: e.reciprocal(out=rs[:], in_=rs[:]), [rs], [rs])
            P.stt(t1_t[:], ps_o[:, :], ng_s[:, 0:1], rs[:], ALU.mult, ALU.mult, [ps_o, ng_s, rs], [t1_t])
            ps_gr = proj_fm(C_GR, 128)
            P.act(sg_t[:], ps_gr[:, :], AF.Silu, [ps_gr], [sg_t])
            P.tt("gpsimd", go_t[:], t1_t[:], sg_t[:], ALU.mult, [t1_t, sg_t], [go_t])
            out_evs.append(P.dma("sync", oT.ap()[0:128, i * TT:(i + 1) * TT], go_t[:], [go_t], []))

        if not do_moba:
            continue
        P.mark('pre_moba')
        ps_mq = proj_fm(C_MQ, 128)
        mqf = mq_f.get(); mqb = mq_b.get()
        P.copy("scalar", mqf[:], ps_mq[:, :], [ps_mq], [mqf])
        P.copy("vector", mqb[:], ps_mq[:, :], [ps_mq], [mqb])
        ps_mk = proj_fm(C_MK, 128)
        P.copy("scalar", kT_all[:, i * TT:(i + 1) * TT], ps_mk[:, :], [ps_mk], [kT_all.s[i]])
        P.op("vector", lambda e, ps_mk=ps_mk, i=i: e.tensor_reduce(
            out=kmean[:, 2 * i:2 * i + 2], in_=ps_mk[:, :].rearrange("p (b t) -> p b t", b=2),
            axis=AX.X, op=ALU.add), [ps_mk], [kmean])
        P.ts("vector", kmean[:, 2 * i:2 * i + 2], kmean[:, 2 * i:2 * i + 2], 1.0 / 256.0, None, ALU.mult, None,
             [kmean], [kmean])
        P.mark('moba_sel')
        sbT = selbT.get()
        for ql in range(4):
            qs = 4 * i + ql
            b = qs // 2
            qsl = slice(ql * 128, (ql + 1) * 128)
            ps_gate = gen.get()
            P.mm(ps_gate[:, 0:32], mqf[:, qsl], kmean[:], True, True, [mqf, kmean], [ps_gate])
            gm_t = gm.get(); t8 = top8.get(); sb_ = selb.get()
            P.tt("vector", gm_t[:], ps_gate[:, 0:32], negpad[:, 32 - b:64 - b], ALU.add, [ps_gate, negpad], [gm_t])
            P.op("vector", lambda e, t8=t8, gm_t=gm_t: e.max(out=t8[:], in_=gm_t[:]), [gm_t], [t8])
            P.ts("vector", sb_[:, 0:32], gm_t[:], t8[:, 2:3], -30000.0, ALU.is_lt, ALU.mult, [gm_t, t8], [sb_])
            ps_tr = gen.get()
            P.tr(ps_tr[:, 0:128], sb_[:], ident_f[:], [sb_, ident_f], [ps_tr])
            P.copy("vector", sbT[0:32, qsl], ps_tr[0:32, 0:128], [ps_tr], [sbT])
        P.mark('moba_att')
        mo = mo_b.get()
        for ql in range(4):
            qs = 4 * i + ql
            b = qs // 2
            qsl = slice(ql * 128, (ql + 1) * 128)
            acc = acc_banks.get()
            pend = []

            def flush(item):
                ks, p_t = item
                P.mm(acc[:, 0:129], p_t[:], V_all[:, ks, 0:129], ks == 0, ks == qs,
                     [p_t, V_all.s[ks // 4]], [acc])

            for g0 in range(0, qs + 1, 4):
                bank = sc_rot.get()
                grp = list(range(g0, min(g0 + 4, qs + 1)))
                for slot, ks in enumerate(grp):
                    n = ks // 2
                    pss = bank[:, slot * 128:(slot + 1) * 128]
                    P.mm(pss, kT_all[:, ks * 128:(ks + 1) * 128], mqb[:, qsl], True, n == b,
                         [kT_all.s[ks // 4], mqb], [bank])
                    if n < b:
                        P.mm(pss, esel[:, n * 128:(n + 1) * 128], sbT[:, qsl], False, True, [esel, sbT], [bank])
                items = []
                for slot, ks in enumerate(grp):
                    pss = bank[:, slot * 128:(slot + 1) * 128]
                    p_t = pT.get()
                    P.act(p_t[:], pss, AF.Exp, [bank, alibi], [p_t], scale=128 ** -0.5,
                          bias=alibi[:, qs - ks:qs - ks + 1])
                    if ks == qs:
                        P.tt("gpsimd", p_t[:], p_t[:], tri_b[:], ALU.mult, [p_t, tri_b], [p_t])
                    items.append((ks, p_t))
                pend.append(items)
                if len(pend) > 1:
                    for it in pend.pop(0):
                        flush(it)
            while pend:
                for it in pend.pop(0):
                    flush(it)
            rd = rden.get(); on = o_n.get()
            P.op("vector", lambda e, rd=rd, acc=acc: e.reciprocal(out=rd[:], in_=acc[:, 128:129]), [acc], [rd])
            P.ts("vector", on[:], acc[:, 0:128], rd[:, 0:1], None, ALU.mult, None, [acc, rd], [on])
            P.tr(trb[:, 0:128], on[:], ident_b[:], [on, ident_b], [trb])
            P.copy("vector", mo[:, qsl], trb[:, 0:128], [trb], [mo])
        out_evs.append(P.dma("sync", oT.ap()[128:256, i * TT:(i + 1) * TT], mo[:], [mo], []))
    for ev in out_evs[-4:]:
        P.wait_event("sync", ev)


import math

OW = 768
O_Q, O_K, O_V, O_Z, O_U, O_BD = 0, 128, 256, 384, 512, 640
NLVL = 7
PI = math.pi


def build_odd(P, x_is_f32, ntiles=NTI, do_gdn=True, do_s5=True, seq=S):
    xT = P.dram("xT", [D, seq], F32 if x_is_f32 else BF16, kind="ExternalInput")
    wh = P.dram("wh", [D, OW], F32, kind="ExternalInput")
    cwT = P.dram("cwT", [128, 12], F32, kind="ExternalInput")
    alog = P.dram("alog", [128, 1], F32, kind="ExternalInput")
    dtb = P.dram("dtb", [128, 1], F32, kind="ExternalInput")
    ng = P.dram("ng", [128, 1], F32, kind="ExternalInput")
    are = P.dram("are", [128, 4], F32, kind="ExternalInput")
    aim = P.dram("aim", [128, 4], F32, kind="ExternalInput")
    lstep = P.dram("lstep", [128, 4], F32, kind="ExternalInput")
    BTr = P.dram("BTr", [128, 512], F32, kind="ExternalInput")
    BTi = P.dram("BTi", [128, 512], F32, kind="ExternalInput")
    CTr = P.dram("CTr", [128, 512], F32, kind="ExternalInput")
    CTi = P.dram("CTi", [128, 512], F32, kind="ExternalInput")
    s5d = P.dram("s5d", [128, 1], F32, kind="ExternalInput")
    c_triu = P.dram("c_triu", [128, 128], F32, kind="ExternalInput")
    c_trius = P.dram("c_trius", [128, 128], F32, kind="ExternalInput")
    c_trils = P.dram("c_trils", [128, 128], F32, kind="ExternalInput")
    c_scan = P.dram("c_scan", [128, TT], F32, kind="ExternalInput")
    c_sel0 = P.dram("c_sel0", [128, 128], F32, kind="ExternalInput")
    c_sel32 = P.dram("c_sel32", [128, 128], F32, kind="ExternalInput")
    c_e2 = P.dram("c_e2", [128, 2], F32, kind="ExternalInput")
    c_ident = P.dram("c_ident", [128, 128], F32, kind="ExternalInput")
    oT = P.dram("oT", [256, seq], BF16, kind="ExternalOutput")

    gen = Banks(P, 4)
    ps_o = P.ps("ps_o", [128, 512], F32)
    ps_y = P.ps("ps_y", [128, 512], F32)
    ps_bc = Rot([P.ps(f"ps_bc{i}", [128, 512], F32) for i in range(2)])

    def cst(name, src, shape):
        t = P.sb(name, shape, F32)
        P.dma("sync", t[:], src.ap(), [], [t])
        return t
    triu = cst("triu", c_triu, [128, 128]); trius = cst("trius", c_trius, [128, 128])
    trils = cst("trils", c_trils, [128, 128]); scanm = cst("scanm", c_scan, [128, TT])
    sel0 = cst("sel0", c_sel0, [128, 128]); sel32 = cst("sel32", c_sel32, [128, 128])
    e2 = cst("e2", c_e2, [128, 2]); ident_f = cst("ident_f", c_ident, [128, 128])
    cw_s = cst("cw_s", cwT, [128, 12]); ng_s = cst("ng_s", ng, [128, 1]); dtb_s = cst("dtb_s", dtb, [128, 1])
    nA = cst("nA", alog, [128, 1]); d_s = cst("d_s", s5d, [128, 1])
    are_s = cst("are_s", are, [128, 4]); aim_s = cst("aim_s", aim, [128, 4]); stp = cst("stp", lstep, [128, 4])
    BTr_s = cst("BTr_s", BTr, [128, 512]); BTi_s = cst("BTi_s", BTi, [128, 512])
    CTr_s = cst("CTr_s", CTr, [128, 512]); CTi_s = cst("CTi_s", CTi, [128, 512])
    ones_f = P.sb("ones_f", [128, 128], F32)
    P.memset("vector", ones_f[:], 1.0, [ones_f])
    P.act(nA[:], nA[:], AF.Exp, [nA], [nA])
    P.ts("vector", nA[:], nA[:], -1.0, None, ALU.mult, None, [nA], [nA])

    w_bf = P.sb("w_bf", [128, KC, OW], BF16)
    wstg = Rot([P.sb(f"wstg{i}", [128, OW], F32) for i in range(2)])
    load_w_cols(P, wh, w_bf, OW, wstg)

    def sbt(name, shape, dt=F32, n=2):
        return Rot([P.sb(f"{name}{i}", shape, dt) for i in range(n)])

    TB = {}
    if do_s5:
        for nm in ("Wre", "Wim", "Pre", "Pim", "PreN", "PimN"):
            TB[nm] = P.sb("tb" + nm, [128, 4, 128], F32)
        Nre = P.sb("Nre", [128, 4, 128], F32); Nim = P.sb("Nim", [128, 4, 128], F32)
        sc4 = lambda nm: P.sb(nm, [128, 4], F32)
        step = sc4("step"); ars = sc4("ars"); ang = sc4("ang"); mag = sc4("mag"); angc = sc4("angc")
        sn = sc4("sn"); cs_ = sc4("cs_"); abr = sc4("abr"); abi = sc4("abi"); tmpa = sc4("tmpa"); tmpb = sc4("tmpb")
        den = sc4("den"); zre = sc4("zre"); zim = sc4("zim"); m2 = sc4("m2"); ivr = sc4("ivr"); ivi = sc4("ivi")
        colt = P.sb("colt", [128, 1], F32)
        P.act(step[:], stp[:], AF.Exp, [stp], [step])
        P.tt("vector", ars[:], are_s[:], step[:], ALU.mult, [are_s, step], [ars])
        P.tt("vector", ang[:], aim_s[:], step[:], ALU.mult, [aim_s, step], [ang])
        P.act(mag[:], ars[:], AF.Exp, [ars], [mag])
        P.ts("vector", angc[:], ang[:], PI / 2, None, ALU.add, None, [ang], [angc])

        def reduce_angle(a):
            for k in range(1, 12, 2):
                P.ts("vector", tmpa[:], a[:], k * PI, 2 * PI, ALU.is_ge, ALU.mult, [a], [tmpa])
                P.tt("vector", tmpb[:], tmpb[:] if k > 1 else a[:], tmpa[:], ALU.subtract,
                     [tmpb, a, tmpa], [tmpb])
            P.copy("vector", a[:], tmpb[:], [tmpb], [a])
        for a in (ang, angc):
            P.memset("vector", tmpb[:], 0.0, [tmpb])
            for k in range(1, 12, 2):
                P.ts("vector", tmpa[:], a[:], k * PI, 2 * PI, ALU.is_ge, ALU.mult, [a], [tmpa])
                P.tt("vector", tmpb[:], tmpb[:], tmpa[:], ALU.add, [tmpb, tmpa], [tmpb])
            P.tt("vector", a[:], a[:], tmpb[:], ALU.subtract, [a, tmpb], [a])
        P.act(sn[:], ang[:], AF.Sin, [ang], [sn])
        P.act(cs_[:], angc[:], AF.Sin, [angc], [cs_])
        P.tt("vector", abr[:], mag[:], cs_[:], ALU.mult, [mag, cs_], [abr])
        P.tt("vector", abi[:], mag[:], sn[:], ALU.mult, [mag, sn], [abi])
        P.tt("vector", den[:], are_s[:], are_s[:], ALU.mult, [are_s], [den])
        P.tt("vector", tmpa[:], aim_s[:], aim_s[:], ALU.mult, [aim_s], [tmpa])
        P.tt("vector", den[:], den[:], tmpa[:], ALU.add, [den, tmpa], [den])
        P.op("vector", lambda e: e.reciprocal(out=den[:], in_=den[:]), [den], [den])
        P.ts("vector", tmpa[:], abr[:], -1.0, None, ALU.add, None, [abr], [tmpa])
        P.tt("vector", zre[:], tmpa[:], are_s[:], ALU.mult, [tmpa, are_s], [zre])
        P.tt("vector", tmpb[:], abi[:], aim_s[:], ALU.mult, [abi, aim_s], [tmpb])
        P.tt("vector", zre[:], zre[:], tmpb[:], ALU.add, [zre, tmpb], [zre])
        P.tt("vector", zre[:], zre[:], den[:], ALU.mult, [zre, den], [zre])
        P.tt("vector", zim[:], abi[:], are_s[:], ALU.mult, [abi, are_s], [zim])
        P.tt("vector", tmpb[:], tmpa[:], aim_s[:], ALU.mult, [tmpa, aim_s], [tmpb])
        P.tt("vector", zim[:], zim[:], tmpb[:], ALU.subtract, [zim, tmpb], [zim])
        P.tt("vector", zim[:], zim[:], den[:], ALU.mult, [zim, den], [zim])
        P.tt("vector", m2[:], abr[:], abr[:], ALU.mult, [abr], [m2])
        P.tt("vector", tmpa[:], abi[:], abi[:], ALU.mult, [abi], [tmpa])
        P.tt("vector", m2[:], m2[:], tmpa[:], ALU.add, [m2, tmpa], [m2])
        P.op("vector", lambda e: e.reciprocal(out=m2[:], in_=m2[:]), [m2], [m2])
        P.tt("vector", ivr[:], abr[:], m2[:], ALU.mult, [abr, m2], [ivr])
        P.tt("vector", ivi[:], abi[:], m2[:], ALU.mult, [abi, m2], [ivi])
        P.ts("vector", ivi[:], ivi[:], -1.0, None, ALU.mult, None, [ivi], [ivi])

        def cmul_scalar(o_re, o_im, i_re, i_im, s_re, s_im, deps_in, deps_out, tmp):
            P.ts("vector", colt[:], s_im, -1.0, None, ALU.mult, None, deps_in, [colt])
            P.ts("vector", tmp, i_re, s_re, None, ALU.mult, None, deps_in, deps_out)
            P.stt(o_re, i_im, colt[:, 0:1], tmp, ALU.mult, ALU.add, deps_in + [colt], deps_out)
            P.ts("vector", tmp, i_re, s_im, None, ALU.mult, None, deps_in, deps_out)
            P.stt(o_im, i_im, s_re, tmp, ALU.mult, ALU.add, deps_in, deps_out)

        ttmp = P.sb("ttmp", [128, 64], F32)
        for (Tre, Tim, b_re, b_im) in ((TB["Pre"], TB["Pim"], abr, abi), (Nre, Nim, ivr, ivi)):
            for sc in range(4):
                P.copy("vector", Tre[:, sc, 0:1], b_re[:, sc:sc + 1], [b_re], [Tre])
                P.copy("vector", Tim[:, sc, 0:1], b_im[:, sc:sc + 1], [b_im], [Tim])
                ln = 1
                while ln < 128:
                    cmul_scalar(Tre[:, sc, ln:2 * ln], Tim[:, sc, ln:2 * ln], Tre[:, sc, 0:ln], Tim[:, sc, 0:ln],
                                Tre[:, sc, ln - 1:ln], Tim[:, sc, ln - 1:ln], [Tre, Tim], [Tre, Tim, ttmp],
                                ttmp[:, 0:ln])
                    ln *= 2
        ttmp2 = P.sb("ttmp2", [128, 128], F32)
        for sc in range(4):
            cmul_scalar(TB["Wre"][:, sc, :], TB["Wim"][:, sc, :], Nre[:, sc, :], Nim[:, sc, :],
                        zre[:, sc:sc + 1], zim[:, sc:sc + 1], [Nre, Nim, zre, zim], [TB["Wre"], TB["Wim"], ttmp2],
                        ttmp2[:, :])
        P.ts("vector", TB["PreN"][:], TB["Pre"][:], -1.0, None, ALU.mult, None, [TB["Pre"]], [TB["PreN"]])
        P.ts("vector", TB["PimN"][:], TB["Pim"][:], -1.0, None, ALU.mult, None, [TB["Pim"]], [TB["PimN"]])
        Xre = P.sb("Xre", [128, 4], F32); Xim = P.sb("Xim", [128, 4], F32)
        P.memset("vector", Xre[:], 0.0, [Xre]); P.memset("vector", Xim[:], 0.0, [Xim])
        uT = sbt("uT", [128, TT]);
        s5t = [sbt(f"s5t{k}", [128, TT], F32, 1) for k in range(4)]
        term = [sbt(f"term{k}", [128, TT], F32, 1) for k in range(2)]
        cc = [sbt(f"cc{k}", [128, TT], F32, 1) for k in range(2)]
        xx = [sbt(f"xx{k}", [128, TT], F32, 2) for k in range(2)]
        ach = [sbt(f"ach{k}", [128, 128], F32, 2) for k in range(2)]
        pch = [sbt(f"pch{k}", [128, 128], F32, 2) for k in range(4)]
        yb = sbt("yb", [128, TT], F32, 1); y2 = sbt("y2", [128, TT], F32, 1); y3 = sbt("y3", [128, TT], F32, 1)
        so = sbt("so", [128, TT], BF16)

    if do_gdn:
        pre = [P.sb(f"pre{k}", [128, 3 + TT], F32) for k in range(3)]
        for p_ in pre:
            P.memset("vector", p_[:], 0.0, [p_])
        cvo = [sbt(f"cvo{k}", [128, TT], F32, 1) for k in range(3)]
        sqb = sbt("sqb", [128, TT], F32, 1); rinv = sbt("rinv", [128, TT], F32, 1)
        qn = sbt("qn", [128, TT], F32, 1); kn = sbt("kn", [128, TT], F32, 1)
        qd_b = sbt("qd_b", [128, TT], BF16, 1); kn_b = sbt("kn_b", [128, TT], BF16, 1); qn_b = sbt("qn_b", [128, TT], BF16, 1)
        R = sbt("R", [128, TT], F32, 1); Rg = sbt("Rg", [128, TT], F32, 1)
        gcb = sbt("gcb", [128, TT], F32, 1); betab = sbt("betab", [128, TT], F32, 1); egb = sbt("egb", [128, TT], F32, 1)
        colv = sbt("colv", [128, 2]); ccol = sbt("ccol", [128, 6])
        Dm = sbt("Dm", [128, 128]); dT = sbt("dT", [128, 128]); dTs = sbt("dTs", [128, 128]); dL = sbt("dL", [128, 128])
        PP = sbt("PP", [128, 256], F32, 3)
        XX = sbt("XX", [128, 256], F32, 3)
        aT = sbt("aT", [128, 128], BF16); ktok = sbt("ktok", [128, 128]); kdec = sbt("kdec", [128, 128], BF16)
        wT_b = sbt("wT_b", [128, 128], BF16); vn_b = sbt("vn_b", [128, 128], BF16)
        S_f = P.sb("S_f", [128, 128], F32); S_b = P.sb("S_b", [128, 128], BF16)
        P.memset("vector", S_f[:], 0.0, [S_f]); P.memset("vector", S_b[:], 0.0, [S_b])
        sq2 = sbt("sq2", [128, TT], F32, 1); rs2 = sbt("rs2", [128, TT], F32, 1); t12 = sbt("t12", [128, TT], F32, 1)
        sg2 = sbt("sg2", [128, TT], F32, 1); go = sbt("go", [128, TT], BF16)

    xbufs = Rot([P.sb(f"xt{i}", [128, KC, TT], BF16, nsub=4) for i in range(2)])
    xstg = Rot([P.sb(f"xs{i}", [128, 4, TT], F32) for i in range(2)]) if x_is_f32 else None
    out_evs = []

    for i in range(ntiles):
        xt = load_x_tile(P, xT, i, x_is_f32, xbufs, xstg)

        def proj_fm(col0):
            ps = gen.get()
            for kc in range(KC):
                P.mm(ps[:, :], w_bf[:, kc, col0:col0 + 128], xt[:, kc, :], kc == 0, kc == KC - 1,
                     [w_bf, xt.s[kc // 4]], [ps])
            return ps

        if do_gdn:
            P.mark(f"gdn{i}")
            for k3, col in enumerate((O_Q, O_K, O_V)):
                ps = proj_fm(col)
                pr = pre[k3]
                P.copy("vector", pr[:, 0:3], pr[:, TT:TT + 3], [pr], [pr])
                P.copy("scalar", pr[:, 3:3 + TT], ps[:, :], [ps], [pr])
                co = cvo[k3].get()
                P.ts("vector", co[:], pr[:, 0:TT], cw_s[:, 4 * k3:4 * k3 + 1], None, ALU.mult, None, [pr, cw_s], [co])
                for j in range(1, 4):
                    P.stt(co[:], pr[:, j:j + TT], cw_s[:, 4 * k3 + j:4 * k3 + j + 1], co[:], ALU.mult, ALU.add,
                          [pr, cw_s, co], [co])
                P.act(co[:], co[:], AF.Silu, [co], [co])
            qc, kc_, vc = cvo[0].items[0], cvo[1].items[0], cvo[2].items[0]
            for src, dst, scale in ((qc, qn.get(), 128 ** -0.5), (kc_, kn.get(), 1.0)):
                sq_t = sqb.get(); ri = rinv.get()
                P.act(sq_t[:], src[:], AF.Square, [src], [sq_t])
                ps = gen.get()
                P.mm(ps[:, :], ones_f[:], sq_t[:], True, True, [ones_f, sq_t], [ps])
                P.ts("vector", ri[:], ps[:, :], RMS_EPS, None, ALU.add, None, [ps], [ri])
                P.act(ri[:], ri[:], AF.Sqrt, [ri], [ri])
                P.op("vector", lambda e, ri=ri: e.reciprocal(out=ri[:], in_=ri[:]), [ri], [ri])
                P.stt(dst[:], src[:], scale, ri[:], ALU.mult, ALU.mult, [src, ri], [dst])
            qn_t, kn_t = qn.items[0], kn.items[0]
            ps_bd = proj_fm(O_BD)
            R_t = R.get(); Rg_t = Rg.get()
            P.copy("vector", R_t[:], ps_bd[:, :], [ps_bd], [R_t])
            P.act(R_t[0:1, :], ps_bd[0:1, :], AF.Sigmoid, [ps_bd], [R_t])
            P.act(Rg_t[32:33, :], ps_bd[32:33, :], AF.Exp, [ps_bd, dtb_s], [Rg_t], bias=dtb_s[32:33, 0:1])
            P.act(Rg_t[32:33, :], Rg_t[32:33, :], AF.Ln, [Rg_t, ones_f], [Rg_t], bias=ones_f[32:33, 0:1])
            P.ts("vector", Rg_t[32:33, :], Rg_t[32:33, :], nA[32:33, 0:1], None, ALU.mult, None, [Rg_t, nA], [Rg_t])
            P.op("vector", lambda e, R_t=R_t, Rg_t=Rg_t: e.tensor_tensor_scan(
                out=R_t[32:33, :], data0=scanm[32:33, :], data1=Rg_t[32:33, :], initial=0.0,
                op0=ALU.mult, op1=ALU.add), [scanm, Rg_t], [R_t])
            gcb_t = gcb.get(); bb_t = betab.get(); eg_t = egb.get()
            pb = ps_bc.get()
            P.mm(pb[:, :], sel32[:], R_t[:], True, True, [sel32, R_t], [pb])
            P.copy("vector", gcb_t[:], pb[:, :], [pb], [gcb_t])
            P.act(eg_t[:], pb[:, :], AF.Exp, [pb], [eg_t])
            pb = ps_bc.get()
            P.mm(pb[:, :], sel0[:], R_t[:], True, True, [sel0, R_t], [pb])
            P.copy("vector", bb_t[:], pb[:, :], [pb], [bb_t])
            qd = qd_b.get(); knb = kn_b.get(); qnb = qn_b.get()
            P.tt("vector", qd[:], qn_t[:], eg_t[:], ALU.mult, [qn_t, eg_t], [qd])
            P.copy("gpsimd", knb[:], kn_t[:], [kn_t], [knb])
            P.copy("gpsimd", qnb[:], qn_t[:], [qn_t], [qnb])
            for j in range(4):
                cs = slice(j * 128, (j + 1) * 128)
                last = j * 128 + 127
                ps = gen.get()
                P.mm(ps[:, 0:2], R_t[:, cs], e2[:], True, True, [R_t, e2], [ps])
                cv = colv.get(); cc_ = ccol.get()
                P.copy("vector", cv[:], ps[:, 0:2], [ps], [cv])
                bcol, gcol = cv[:, 0:1], cv[:, 1:2]
                glast = gcb_t[:, last:last + 1]
                P.act(cc_[:, 0:1], gcol, AF.Exp, [cv], [cc_])
                P.tt("vector", cc_[:, 1:2], cc_[:, 0:1], bcol, ALU.mult, [cc_, cv], [cc_])
                P.act(cc_[:, 2:3], gcol, AF.Exp, [cv, gcb_t], [cc_], scale=-1.0, bias=glast)
                P.act(cc_[:, 3:4], glast, AF.Exp, [gcb_t], [cc_])
                Dm_t = Dm.get(); dT_t = dT.get(); dTs_t = dTs.get(); dL_t = dL.get()
                P.ts("vector", Dm_t[:], gcb_t[:, cs], gcol, None, ALU.subtract, None, [gcb_t, cv], [Dm_t])
                P.ts("vector", dT_t[:], Dm_t[:], 0.0, None, ALU.min, None, [Dm_t], [dT_t])
                P.act(dT_t[:], dT_t[:], AF.Exp, [dT_t], [dT_t])
                P.tt("gpsimd", dTs_t[:], dT_t[:], trius[:], ALU.mult, [dT_t, trius], [dTs_t])
                P.tt("gpsimd", dT_t[:], dT_t[:], triu[:], ALU.mult, [dT_t, triu], [dT_t])
                P.ts("vector", dL_t[:], Dm_t[:], 0.0, None, ALU.max, None, [Dm_t], [dL_t])
                P.act(dL_t[:], dL_t[:], AF.Exp, [dL_t], [dL_t], scale=-1.0)
                P.tt("gpsimd", dL_t[:], dL_t[:], trils[:], ALU.mult, [dL_t, trils], [dL_t])
                ps_kk = gen.get()
                P.mm(ps_kk[:, 0:128], kn_t[:, cs], kn_t[:, cs], True, True, [kn_t], [ps_kk])
                P.mm(ps_kk[:, 128:256], knb[:, cs], qnb[:, cs], True, True, [knb, qnb], [ps_kk])
                pp = PP.get()
                P.tt("vector", pp[:, 0:128], ps_kk[:, 0:128], dL_t[:], ALU.mult, [ps_kk, dL_t], [pp])
                P.ts("vector", pp[:, 0:128], pp[:, 0:128], bcol, -1.0, ALU.mult, ALU.mult, [pp, cv], [pp])
                P.tt("vector", pp[:, 128:256], ps_kk[:, 0:128], dTs_t[:], ALU.mult, [ps_kk, dTs_t], [pp])
                P.stt(pp[:, 128:256], pp[:, 128:256], -1.0, bb_t[:, cs], ALU.mult, ALU.mult, [pp, bb_t], [pp])
                at = aT.get()
                P.tt("vector", at[:], ps_kk[:, 128:256], dT_t[:], ALU.mult, [ps_kk, dT_t], [at])
                ps_t = gen.get()
                P.tr(ps_t[:, 0:128], kn_t[:, cs], ident_f[:], [kn_t, ident_f], [ps_t])
                P.tr(ps_t[:, 128:256], vc[:, cs], ident_f[:], [vc, ident_f], [ps_t])
                X = XX.get()
                kd = kdec.get()
                P.ts("vector", X[:, 0:128], ps_t[:, 128:256], bcol, None, ALU.mult, None, [ps_t, cv], [X])
                P.ts("vector", X[:, 128:256], ps_t[:, 0:128], cc_[:, 1:2], None, ALU.mult, None, [ps_t, cc_], [X])
                P.ts("vector", kd[:], ps_t[:, 0:128], cc_[:, 2:3], None, ALU.mult, None, [ps_t, cc_], [kd])
                for lvl in range(NLVL):
                    psX = gen.get()
                    P.mm(psX[:, 0:256], pp[:, 128:256], X[:, :], True, True, [pp, X], [psX])
                    if lvl < NLVL - 1:
                        psP = gen.get()
                        P.mm(psP[:, 0:128], pp[:, 128:256], pp[:, 0:128], True, True, [pp], [psP])
                        P.mm(psP[:, 128:256], pp[:, 0:128], pp[:, 128:256], True, True, [pp], [psP])
                    Xn = XX.get()
                    P.tt("vector", Xn[:], X[:], psX[:, 0:256], ALU.add, [X, psX], [Xn])
                    if lvl < NLVL - 1:
                        pn = PP.get()
                        P.copy("scalar", pn[:], psP[:, 0:256], [psP], [pn])
                        pp = pn
                    X = Xn
                ps_w = gen.get()
                P.tr(ps_w[:, 0:128], X[:, 128:256], ident_f[:], [X, ident_f], [ps_w])
                wt = wT_b.get()
                P.copy("vector", wt[:], ps_w[:, 0:128], [ps_w], [wt])
                ps_ws = gen.get()
                P.mm(ps_ws[:, 0:128], wt[:], S_b[:], True, True, [wt, S_b], [ps_ws])
                vn = vn_b.get()
                P.tt("vector", vn[:], X[:, 0:128], ps_ws[:, 0:128], ALU.subtract, [X, ps_ws], [vn])
                P.mm(ps_o[:, cs], S_b[:], qd[:, cs], True, False, [S_b, qd], [ps_o])
                P.mm(ps_o[:, cs], vn[:], at[:], False, True, [vn, at], [ps_o])
                ps_kv = gen.get()
                P.mm(ps_kv[:, 0:128], kd[:], vn[:], True, True, [kd, vn], [ps_kv])
                P.stt(S_f[:], S_f[:], cc_[:, 3:4], ps_kv[:, 0:128], ALU.mult, ALU.add, [S_f, cc_, ps_kv], [S_f])
                P.copy("vector", S_b[:], S_f[:], [S_f], [S_b])
            sq_t = sq2.get(); rs = rs2.get(); t1_t = t12.get(); sg_t = sg2.get(); go_t = go.get()
            P.act(sq_t[:], ps_o[:, :], AF.Square, [ps_o], [sq_t])
            ps_ms = gen.get()
            P.mm(ps_ms[:, :], ones_f[:], sq_t[:], True, True, [ones_f, sq_t], [ps_ms])
            P.ts("vector", rs[:], ps_ms[:, :], 1.0 / 128.0, RMS_EPS, ALU.mult, ALU.add, [ps_ms], [rs])
            P.act(rs[:], rs[:], AF.Sqrt, [rs], [rs])
            P.op("vector", lambda e, rs=rs: e.reciprocal(out=rs[:], in_=rs[:]), [rs], [rs])
            P.stt(t1_t[:], ps_o[:, :], ng_s[:, 0:1], rs[:], ALU.mult, ALU.mult, [ps_o, ng_s, rs], [t1_t])
            ps_z = proj_fm(O_Z)
            P.act(sg_t[:], ps_z[:, :], AF.Silu, [ps_z], [sg_t])
            P.tt("gpsimd", go_t[:], t1_t[:], sg_t[:], ALU.mult, [t1_t, sg_t], [go_t])
            out_evs.append(P.dma("sync", oT.ap()[0:128, i * TT:(i + 1) * TT], go_t[:], [go_t], []))

        if do_s5:
            P.mark(f"s5_{i}")
            ps_u = proj_fm(O_U)
            u_t = uT.get()
            P.copy("scalar", u_t[:], ps_u[:, :], [ps_u], [u_t])
            for sc in range(4):
                pr_ = gen.get(); pi_ = gen.get()
                P.mm(pr_[:, :], BTr_s[:, sc * 128:(sc + 1) * 128], u_t[:], True, True, [BTr_s, u_t], [pr_])
                P.mm(pi_[:, :], BTi_s[:, sc * 128:(sc + 1) * 128], u_t[:], True, True, [BTi_s, u_t], [pi_])
                t = [s5t[k].get() for k in range(4)]
                tre, tim = term[0].get(), term[1].get()
                for j in range(4):
                    cs = slice(j * 128, (j + 1) * 128)
                    P.tt("vector", t[0][:, cs], pr_[:, cs], TB["Wre"][:, sc, :], ALU.mult, [pr_, TB["Wre"]], [t[0]])
                    P.tt("vector", t[1][:, cs], pi_[:, cs], TB["Wim"][:, sc, :], ALU.mult, [pi_, TB["Wim"]], [t[1]])
                    P.tt("vector", t[2][:, cs], pi_[:, cs], TB["Wre"][:, sc, :], ALU.mult, [pi_, TB["Wre"]], [t[2]])
                    P.tt("vector", t[3][:, cs], pr_[:, cs], TB["Wim"][:, sc, :], ALU.mult, [pr_, TB["Wim"]], [t[3]])
                P.tt("gpsimd", tre[:], t[0][:], t[1][:], ALU.subtract, [t[0], t[1]], [tre])
                P.tt("gpsimd", tim[:], t[2][:], t[3][:], ALU.add, [t[2], t[3]], [tim])
                cre, cim = cc[0].get(), cc[1].get()
                P.op("vector", lambda e, cre=cre, tre=tre: e.tensor_tensor_scan(
                    out=cre[:], data0=scanm[:], data1=tre[:], initial=0.0, op0=ALU.mult, op1=ALU.add),
                    [scanm, tre], [cre])
                P.op("vector", lambda e, cim=cim, tim=tim: e.tensor_tensor_scan(
                    out=cim[:], data0=scanm[:], data1=tim[:], initial=0.0, op0=ALU.mult, op1=ALU.add),
                    [scanm, tim], [cim])
                xr, xi = xx[0].get(), xx[1].get()
                for j in range(4):
                    cs = slice(j * 128, (j + 1) * 128)
                    a_r, a_i = ach[0].get(), ach[1].get()
                    p_ = [pch[k].get() for k in range(4)]
                    P.ts("vector", a_r[:], cre[:, cs], Xre[:, sc:sc + 1], None, ALU.add, None, [cre, Xre], [a_r])
                    P.ts("vector", a_i[:], cim[:, cs], Xim[:, sc:sc + 1], None, ALU.subtract, None, [cim, Xim], [a_i])
                    P.tt("gpsimd", p_[0][:], a_r[:], TB["Pre"][:, sc, :], ALU.mult, [a_r, TB["Pre"]], [p_[0]])
                    P.tt("gpsimd", p_[1][:], a_i[:], TB["Pim"][:, sc, :], ALU.mult, [a_i, TB["Pim"]], [p_[1]])
                    P.tt("vector", p_[2][:], a_r[:], TB["PimN"][:, sc, :], ALU.mult, [a_r, TB["PimN"]], [p_[2]])
                    P.tt("vector", p_[3][:], a_i[:], TB["PreN"][:, sc, :], ALU.mult, [a_i, TB["PreN"]], [p_[3]])
                    P.tt("gpsimd", xr[:, cs], p_[0][:], p_[1][:], ALU.subtract, [p_[0], p_[1]], [xr])
                    P.tt("vector", xi[:, cs], p_[2][:], p_[3][:], ALU.add, [p_[2], p_[3]], [xi])
                    P.copy("vector", Xre[:, sc:sc + 1], xr[:, j * 128 + 127:j * 128 + 128], [xr], [Xre])
                    P.copy("vector", Xim[:, sc:sc + 1], xi[:, j * 128 + 127:j * 128 + 128], [xi], [Xim])
                P.mm(ps_y[:, :], CTr_s[:, sc * 128:(sc + 1) * 128], xr[:], sc == 0, False, [CTr_s, xr], [ps_y])
                P.mm(ps_y[:, :], CTi_s[:, sc * 128:(sc + 1) * 128], xi[:], False, sc == 3, [CTi_s, xi], [ps_y])
            y = yb.get(); y2_ = y2.get(); y3_ = y3.get(); so_t = so.get()
            P.stt(y[:], u_t[:], d_s[:, 0:1], ps_y[:, :], ALU.mult, ALU.add, [u_t, d_s, ps_y], [y])
            P.tt("vector", y2_[:], y[:], y[:], ALU.mult, [y], [y2_])
            P.ts("vector", y2_[:], y2_[:], 0.044715, 1.0, ALU.mult, ALU.add, [y2_], [y2_])
            P.tt("vector", y3_[:], y2_[:], y[:], ALU.mult, [y2_, y], [y3_])
            P.act(y3_[:], y3_[:], AF.Tanh, [y3_], [y3_], scale=math.sqrt(2.0 / PI))
            P.ts("vector", y3_[:], y3_[:], 1.0, 0.5, ALU.add, ALU.mult, [y3_], [y3_])
            P.tt("gpsimd", so_t[:], y3_[:], y[:], ALU.mult, [y3_, y], [so_t])
            out_evs.append(P.dma("sync", oT.ap()[128:256, i * TT:(i + 1) * TT], so_t[:], [so_t], []))
    for ev in out_evs[-4:]:
        P.wait_event("sync", ev)

import numpy as np

EV_OFF = dict(gq=0, gk=512, gv=1024, glr=2048, gr=2064, mq=3088, mk=4112, mv=5136)


def even_consts(h):
    tri = (np.arange(128)[:, None] <= np.arange(128)[None, :]).astype(np.float32)
    scan = np.ones((64, 512), np.float32); scan[:, ::128] = 0.0
    negpad = np.concatenate([np.zeros((128, 32), np.float32), np.full((128, 32), -1e30, np.float32)], 1)
    esel = np.zeros((128, 32, 128), np.float32)
    for n in range(32):
        esel[n, n, :] = 1.0
    slope = 2.0 ** (-(h + 1))
    alibi = (slope * (np.arange(128, dtype=np.float64)[:, None] - 128.0 * np.arange(64)[None, :])).astype(np.float32)
    return dict(c_tri=tri, c_scan=scan, c_negpad=negpad, c_esel=esel.reshape(128, 4096), c_alibi=alibi,
                c_ident=np.eye(128, dtype=np.float32))


def even_inputs(h, w_in, w_gate2, b_gate, norm_g):
    o = EV_OFF
    cols = np.concatenate([
        np.arange(o['gq'] + h * 64, o['gq'] + (h + 1) * 64), np.arange(o['gk'] + h * 64, o['gk'] + (h + 1) * 64),
        np.arange(o['gr'] + h * 128, o['gr'] + (h + 1) * 128), np.arange(o['mq'] + h * 128, o['mq'] + (h + 1) * 128),
        np.arange(o['mk'] + h * 128, o['mk'] + (h + 1) * 128), np.arange(o['glr'], o['glr'] + 16),
        np.arange(o['gv'] + h * 128, o['gv'] + (h + 1) * 128), np.arange(o['mv'] + h * 128, o['mv'] + (h + 1) * 128)])
    wg2p = np.zeros((128, 128), np.float32)
    wg2p[0:16, 0:64] = w_gate2[:, h * 64:(h + 1) * 64]
    d = dict(wh=np.ascontiguousarray(w_in[:, cols]), wg2=wg2p,
             bg=np.ascontiguousarray(b_gate[h * 64:(h + 1) * 64].reshape(64, 1)),
             ng=np.ascontiguousarray(norm_g.reshape(128, 1)))
    d.update(even_consts(h))
    return d


OD_OFF = dict(q=0, k=1024, v=2048, z=3072, beta=4096, decay=4104, u=4112)


def odd_consts():
    p = np.arange(128)
    triu = (p[:, None] <= p[None, :]).astype(np.float32)
    trius = (p[:, None] < p[None, :]).astype(np.float32)
    trils = (p[:, None] > p[None, :]).astype(np.float32)
    scan = np.ones((128, 512), np.float32); scan[:, ::128] = 0.0
    sel0 = np.zeros((128, 128), np.float32); sel0[0, :] = 1.0
    sel32 = np.zeros((128, 128), np.float32); sel32[32, :] = 1.0
    e2 = np.zeros((128, 2), np.float32); e2[0, 0] = 1.0; e2[32, 1] = 1.0
    return dict(c_triu=triu, c_trius=trius, c_trils=trils, c_scan=scan, c_sel0=sel0, c_sel32=sel32, c_e2=e2,
                c_ident=np.eye(128, dtype=np.float32))


def odd_inputs(h, w_in, conv_w, a_log, dt_bias, norm_g, a_re, a_im, b_re, b_im, c_re, c_im, d, log_step):
    o = OD_OFF
    wh = np.zeros((2048, 768), np.float32)
    for k, nm in enumerate(("q", "k", "v", "z", "u")):
        wh[:, k * 128:(k + 1) * 128] = w_in[:, o[nm] + h * 128:o[nm] + (h + 1) * 128]
    wh[:, 640] = w_in[:, o["beta"] + h]
    wh[:, 672] = w_in[:, o["decay"] + h]
    cwT = np.zeros((128, 12), np.float32)
    for k3 in range(3):
        cwT[:, 4 * k3:4 * k3 + 4] = conv_w[:, k3 * 1024 + h * 128:k3 * 1024 + (h + 1) * 128].T
    rep = lambda v: np.full((128, 1), v, np.float32)
    gs = np.arange(8 * h, 8 * h + 8)
    def st(a):
        return np.ascontiguousarray(a[gs].reshape(4, 2, 64).transpose(1, 2, 0).reshape(128, 4))
    lst = np.ascontiguousarray(np.repeat(log_step[gs].reshape(4, 2, 1), 64, axis=2).transpose(1, 2, 0).reshape(128, 4))
    BTr = np.zeros((128, 4, 128), np.float32); BTi = np.zeros((128, 4, 128), np.float32)
    CTr = np.zeros((128, 4, 128), np.float32); CTi = np.zeros((128, 4, 128), np.float32)
    for gl in range(8):
        sc, g2 = gl // 2, gl % 2
        g = 8 * h + gl
        BTr[gl * 16:(gl + 1) * 16, sc, g2 * 64:(g2 + 1) * 64] = b_re[g].T
        BTi[gl * 16:(gl + 1) * 16, sc, g2 * 64:(g2 + 1) * 64] = b_im[g].T
        CTr[g2 * 64:(g2 + 1) * 64, sc, gl * 16:(gl + 1) * 16] = c_re[g].T
        CTi[g2 * 64:(g2 + 1) * 64, sc, gl * 16:(gl + 1) * 16] = c_im[g].T
    dd = dict(wh=wh, cwT=cwT, alog=rep(a_log[h]), dtb=rep(dt_bias[h]), ng=np.ascontiguousarray(norm_g.reshape(128, 1)),
              are=st(a_re), aim=st(a_im), lstep=lst, BTr=BTr.reshape(128, 512), BTi=BTi.reshape(128, 512),
              CTr=CTr.reshape(128, 512), CTi=CTi.reshape(128, 512),
              s5d=np.ascontiguousarray(d[gs].reshape(128, 1)))
    dd.update(odd_consts())
    return dd

from concourse.bass_utils import run_bass_kernel_spmd
import ml_dtypes

NCORES = 8
DEPTH = 4
RUN_MIXERS = True


def _fm(v):
    return np.ascontiguousarray(np.asarray(v, np.float32).reshape(KC, 128).T)


def _post_inputs(c, oT_full, xT_full, w_out, lnp, w_up, cwl, w_down, extra):
    s0 = c * NTOK
    if c == 0:
        oT = np.concatenate([np.zeros((D, 2), oT_full.dtype), oT_full[:, 0:NTOK]], axis=1)
        xT = np.concatenate([np.zeros((D, 2), np.float32), xT_full[:, 0:NTOK]], axis=1)
    else:
        oT = oT_full[:, s0 - 2:s0 + NTOK]
        xT = xT_full[:, s0 - 2:s0 + NTOK]
    d = dict(oT=np.ascontiguousarray(oT), xT=np.ascontiguousarray(xT),
             flag=np.full((128, 1), 0.0 if c == 0 else 1.0, np.float32),
             w_out=w_out, lnp=lnp, w_up=w_up, cw=cwl, w_down=w_down)
    d.update(extra)
    return d


_PROGS = {}


def _prog(key):
    if key not in _PROGS:
        nc = bass.Bass("TRN2", target_bir_lowering=False)
        P = Prog(nc)
        if key == "post":
            build_post(P, False)
        elif key == "post_glu":
            build_post(P, True)
        elif key == "even_f32":
            build_even(P, True)
        elif key == "even":
            build_even(P, False)
        elif key == "odd":
            build_odd(P, False)
        P.emit()
        _PROGS[key] = nc
    return _PROGS[key]


def kernel(x, even_w_in, gla_w_gate2, gla_b_gate, gla_norm_g, even_w_out, odd_w_in, gdn_conv_w, gdn_a_log,
           gdn_dt_bias, gdn_norm_g, s5_a_re, s5_a_im, s5_b_re, s5_b_im, s5_c_re, s5_c_im, s5_d, s5_log_step,
           s5_glu_w, s5_glu_b, odd_w_out, ln_mix_g, ln_mix_b, ffn_w_up, ffn_conv_w, ffn_w_down, ln_ffn_g, ln_ffn_b):
    f32 = lambda a: np.asarray(a, np.float32)
    x = f32(x)
    xT_full = np.ascontiguousarray(x[0].T)
    xT_bf = None
    cores = list(range(NCORES))
    for i in range(DEPTH):
        j = i // 2
        if not RUN_MIXERS:
            res = None
            w_out = f32(even_w_out[j] if i % 2 == 0 else odd_w_out[j])
            extra = {} if i % 2 == 0 else dict(glu_w=f32(s5_glu_w[j]),
                                               glu_b=np.ascontiguousarray(f32(s5_glu_b[j]).reshape(8, 128).T))
            pkey = "post" if i % 2 == 0 else "post_glu"
        elif i % 2 == 0:
            in_maps = []
            for h in cores:
                d = even_inputs(h, f32(even_w_in[j]), f32(gla_w_gate2[j]), f32(gla_b_gate[j]), f32(gla_norm_g[j]))
                d["xT"] = xT_full if xT_bf is None else xT_bf
                in_maps.append(d)
            res = run_bass_kernel_spmd(_prog("even_f32" if xT_bf is None else "even"), in_maps, core_ids=cores)
            w_out = f32(even_w_out[j])
            extra = {}
            pkey = "post"
        else:
            in_maps = []
            for h in cores:
                d = odd_inputs(h, f32(odd_w_in[j]), f32(gdn_conv_w[j]), f32(gdn_a_log[j]), f32(gdn_dt_bias[j]),
                               f32(gdn_norm_g[j]), f32(s5_a_re[j]), f32(s5_a_im[j]), f32(s5_b_re[j]), f32(s5_b_im[j]),
                               f32(s5_c_re[j]), f32(s5_c_im[j]), f32(s5_d[j]), f32(s5_log_step[j]))
                d["xT"] = xT_bf
                in_maps.append(d)
            res = run_bass_kernel_spmd(_prog("odd"), in_maps, core_ids=cores)
            w_out = f32(odd_w_out[j])
            extra = dict(glu_w=f32(s5_glu_w[j]), glu_b=np.ascontiguousarray(f32(s5_glu_b[j]).reshape(8, 128).T))
            pkey = "post_glu"
        if res is None:
            oT_full = np.zeros((D, x.shape[1]), ml_dtypes.bfloat16)
        else:
            oT_full = np.concatenate([res.results[h]["oT"][0:128] for h in cores] +
                                     [res.results[h]["oT"][128:256] for h in cores], axis=0)
        lnp = np.stack([_fm(ln_mix_g[i]), _fm(ln_mix_b[i]), _fm(ln_ffn_g[i]), _fm(ln_ffn_b[i])], axis=1)
        cwl = np.ascontiguousarray(f32(ffn_conv_w[i]).reshape(3, 2 * FC, 128).transpose(2, 0, 1))
        in_maps = [_post_inputs(c, oT_full, xT_full, w_out, lnp, f32(ffn_w_up[i]), cwl, f32(ffn_w_down[i]), extra)
                   for c in cores]
        res = run_bass_kernel_spmd(_prog(pkey), in_maps, core_ids=cores)
        xT_full = np.concatenate([res.results[c]["xo"] for c in cores], axis=1)
        xT_bf = np.concatenate([res.results[c]["xob"] for c in cores], axis=1)
    return np.ascontiguousarray(xT_full.T)[None].astype(np.float32)
```
